# Optimizing a Trainium2 kernel written in Bass

```python
import math
import jax, jax.numpy as jnp
from jax import lax
import numpy as np

D_MODEL = 1024
BATCH = 8
SEQ = 4096
DEPTH = 1

HEAD_DIM = 64
N_ATTN_HEADS = D_MODEL // (2 * HEAD_DIM)
ATTN_WIDTH = N_ATTN_HEADS * 2 * HEAD_DIM
CONV_WIDTH = D_MODEL
CONV_K = 3
D_FF = 2816
ROPE_THETA = 10000.0
Q_BLOCK = 128
EPS = 1e-6
SUBLN_EPS = 1e-5
IN_COLS = 3 * ATTN_WIDTH + 3 * CONV_WIDTH

kernel_name = "hybrid_diffattn_shortconv_convffn_adaln"


def rms_norm(x, g, eps=EPS):
    x32 = x.astype(jnp.float32)
    y = x32 * lax.rsqrt(jnp.mean(x32 * x32, axis=-1, keepdims=True) + eps)
    return (y * g.astype(jnp.float32)).astype(x.dtype)


def modulate(n, shift, scale):
    return n * (1.0 + scale) + shift


def rope_tables(positions):
    inv_freq = ROPE_THETA ** (-jnp.arange(0, HEAD_DIM, 2, dtype=jnp.float32) / HEAD_DIM)
    ang = positions.astype(jnp.float32)[..., None] * inv_freq
    return jnp.cos(ang), jnp.sin(ang)


def apply_rope(t, cos, sin):
    t32 = t.astype(jnp.float32)
    c = cos[:, :, None, None, :]
    s = sin[:, :, None, None, :]
    t1, t2 = jnp.split(t32, 2, axis=-1)
    return jnp.concatenate([t1 * c - t2 * s, t2 * c + t1 * s], axis=-1).astype(t.dtype)


def depthwise_conv3(u, w):
    up = jnp.pad(u, ((0, 0), (1, 1), (0, 0)))
    return up[:, :-2] * w[0] + up[:, 1:-1] * w[1] + up[:, 2:] * w[2]


def diff_attention(q, k, v, lam):
    b_, s_, h_, _, dh = q.shape
    nblk = s_ // Q_BLOCK
    qb = q.reshape(b_, nblk, Q_BLOCK, h_, 2, dh).swapaxes(0, 1)
    scale = dh ** -0.5

    def block(q_blk):
        s = jnp.einsum('bqhcd,bkhcd->bhcqk', q_blk, k).astype(jnp.float32) * scale
        p = jax.nn.softmax(s, axis=-1)
        p_diff = p[:, :, 0] - lam * p[:, :, 1]
        return jnp.einsum('bhqk,bkhe->bqhe', p_diff.astype(v.dtype), v)

    out = lax.map(block, qb)
    return out.swapaxes(0, 1).reshape(b_, s_, h_, 2 * dh)


def hybrid_mixer(h, cos, sin, w_in, conv_w, lq1, lk1, lq2, lk2, subln_g,
                 w_attn_o, w_conv_o, w_gate, b_gate, w_out, lambda_init):
    b_, s_, _ = h.shape
    proj = h @ w_in
    a0 = ATTN_WIDTH
    q, k, v, gb, gc, u = jnp.split(
        proj, [a0, 2 * a0, 3 * a0, 3 * a0 + CONV_WIDTH, 3 * a0 + 2 * CONV_WIDTH], axis=-1)
    q = apply_rope(q.reshape(b_, s_, N_ATTN_HEADS, 2, HEAD_DIM), cos, sin)
    k = apply_rope(k.reshape(b_, s_, N_ATTN_HEADS, 2, HEAD_DIM), cos, sin)
    v = v.reshape(b_, s_, N_ATTN_HEADS, 2 * HEAD_DIM)
    f32 = jnp.float32
    lam = (jnp.exp(jnp.sum(lq1.astype(f32) * lk1.astype(f32)))
           - jnp.exp(jnp.sum(lq2.astype(f32) * lk2.astype(f32))) + lambda_init)
    attn = diff_attention(q, k, v, lam)
    attn = rms_norm(attn, subln_g, SUBLN_EPS) * (1.0 - lambda_init)
    y_a = attn.reshape(b_, s_, ATTN_WIDTH) @ w_attn_o
    y_b = (gb * depthwise_conv3(gc * u, conv_w)) @ w_conv_o
    gate_a, gate_b = jnp.split(jax.nn.sigmoid(h @ w_gate + b_gate), 2, axis=-1)
    return (gate_a * y_a + gate_b * y_b) @ w_out


def conv_ffn(h, w_up, conv_w, conv_b, w_down):
    a, b = jnp.split(h @ w_up, 2, axis=-1)
    a = depthwise_conv3(a, conv_w) + conv_b
    return (jax.nn.silu(a) * b) @ w_down


def setup_inputs(seed: int = 0) -> dict:
    key = jax.random.key(seed)
    ks = jax.random.split(key, 26)
    L, D = DEPTH, D_MODEL
    f32 = jnp.float32

    def nrm(k, shape, fan_in, mult=1.0):
        return jax.random.normal(k, shape, f32) * (mult * fan_in ** -0.5)

    def gain(k, shape):
        return 1.0 + 0.05 * jax.random.normal(k, shape, f32)

    def small(k, shape, s=0.02):
        return s * jax.random.normal(k, shape, f32)

    return {
        "x": jax.random.normal(ks[0], (BATCH, SEQ, D), f32),
        "c": jax.random.normal(ks[1], (BATCH, D), f32),
        "positions": jnp.broadcast_to(jnp.arange(SEQ, dtype=jnp.int32), (BATCH, SEQ)),
        "w_ada": nrm(ks[2], (L, D, 6 * D), D, 0.5),
        "b_ada": small(ks[3], (L, 6 * D)),
        "norm1_g": gain(ks[4], (L, D)),
        "w_in": nrm(ks[5], (L, D, IN_COLS), D),
        "conv_w": nrm(ks[6], (L, CONV_K, CONV_WIDTH), CONV_K),
        "lambda_q1": small(ks[7], (L, HEAD_DIM), 0.1),
        "lambda_k1": small(ks[8], (L, HEAD_DIM), 0.1),
        "lambda_q2": small(ks[9], (L, HEAD_DIM), 0.1),
        "lambda_k2": small(ks[10], (L, HEAD_DIM), 0.1),
        "subln_g": gain(ks[11], (L, 2 * HEAD_DIM)),
        "w_attn_o": nrm(ks[12], (L, ATTN_WIDTH, D), ATTN_WIDTH),
        "w_conv_o": nrm(ks[13], (L, CONV_WIDTH, D), CONV_WIDTH),
        "w_gate": nrm(ks[14], (L, D, 2 * D), D),
        "b_gate": small(ks[15], (L, 2 * D)),
        "w_out": nrm(ks[16], (L, D, D), D),
        "norm2_g": gain(ks[17], (L, D)),
        "w_up": nrm(ks[18], (L, D, 2 * D_FF), D),
        "ffn_conv_w": nrm(ks[19], (L, CONV_K, D_FF), CONV_K),
        "ffn_conv_b": small(ks[20], (L, D_FF)),
        "w_down": nrm(ks[21], (L, D_FF, D), D_FF),
        "final_g": gain(ks[22], (D,)),
    }


def reference(x, c, positions, w_ada, b_ada, norm1_g, w_in, conv_w, lambda_q1, lambda_k1,
              lambda_q2, lambda_k2, subln_g, w_attn_o, w_conv_o, w_gate, b_gate, w_out,
              norm2_g, w_up, ffn_conv_w, ffn_conv_b, w_down, final_g):
    cos, sin = rope_tables(positions)
    c_act = jax.nn.silu(c)
    for layer in range(DEPTH):
        lambda_init = 0.8 - 0.6 * math.exp(-0.3 * layer)
        mod = (c_act @ w_ada[layer] + b_ada[layer])[:, None, :]
        sh1, sc1, g1, sh2, sc2, g2 = jnp.split(mod, 6, axis=-1)
        h = modulate(rms_norm(x, norm1_g[layer]), sh1, sc1)
        mix = hybrid_mixer(h, cos, sin, w_in[layer], conv_w[layer],
                           lambda_q1[layer], lambda_k1[layer], lambda_q2[layer], lambda_k2[layer],
                           subln_g[layer], w_attn_o[layer], w_conv_o[layer],
                           w_gate[layer], b_gate[layer], w_out[layer], lambda_init)
        x = x + g1 * mix
        h = modulate(rms_norm(x, norm2_g[layer]), sh2, sc2)
        x = x + g2 * conv_ffn(h, w_up[layer], ffn_conv_w[layer], ffn_conv_b[layer], w_down[layer])
    return rms_norm(x, final_g)
```

```python
import math
import numpy as np
import concourse.bass as bass
import concourse.mybir as mybir
from concourse.bass_utils import run_bass_kernel_spmd

F32 = mybir.dt.float32
BF16 = mybir.dt.bfloat16
I32 = mybir.dt.int32
AF = mybir.ActivationFunctionType
ALU = mybir.AluOpType

D = 1024
KC = 8
DFF = 2816
FC = 22
EPS = 1e-6
SUBLN_EPS = 1e-5
LAMBDA_INIT = 0.8 - 0.6 * math.exp(-0.3 * 0)
TB = 512

P_BADA = 0
P_N1G = P_BADA + 48
P_N2G = P_N1G + 8
P_FG = P_N2G + 8
P_CONVW = P_FG + 8
P_FCONVW = P_CONVW + 24
P_FCONVB = P_FCONVW + 66
P_BGATE = P_FCONVB + 22
P_SUBLN = P_BGATE + 16
P_LAM = P_SUBLN + 1
P_ROPE = P_LAM + 256
P_C = P_ROPE + 4
P_TOT = P_C + 8


class Sched:
    def __init__(self, nc):
        self.nc = nc
        self.eng = {"pe": nc.tensor, "act": nc.scalar, "dve": nc.vector, "pool": nc.gpsimd, "sp": nc.sync}
        self.sem = {k: nc.alloc_semaphore("sem_" + k) for k in self.eng}
        self.cnt = {k: 0 for k in self.eng}
        self.waited = {}
        self.lastw = {}
        self.readers = {}
        self.dsem = {}

    def _wait(self, e, tok, is_dma=False):
        sem, val, src = tok
        if src == e and not is_dma and e == "pe":
            return
        k = (e, sem.num)
        if self.waited.get(k, 0) >= val:
            return
        self.eng[e].wait_ge(sem, val)
        self.waited[k] = val

    def _deps(self, e, reads, writes, is_dma=False):
        for r in reads:
            t = self.lastw.get(r)
            if t is not None:
                self._wait(e, t, is_dma)
        for w in writes:
            t = self.lastw.get(w)
            if t is not None:
                self._wait(e, t, is_dma)
            for t in self.readers.get(w, {}).values():
                self._wait(e, t, is_dma)

    def _record(self, tok, reads, writes):
        for w in writes:
            self.lastw[w] = tok
            self.readers[w] = {}
        for r in reads:
            d = self.readers.setdefault(r, {})
            k = tok[0].num
            if k not in d or d[k][1] < tok[1]:
                d[k] = tok

    def op(self, e, fn, reads=(), writes=(), inc=True):
        pr = [r for r in reads if isinstance(r, tuple) and r[0] == "ps"]
        if pr:
            reads = [r for r in reads if not (isinstance(r, tuple) and r[0] == "ps")]
            writes = list(writes) + pr
        self._deps(e, reads, writes)
        ins = fn(self.eng[e])
        if inc:
            ins.then_inc(self.sem[e], 1)
            self.cnt[e] += 1
            val = self.cnt[e]
        else:
            val = self.cnt[e] + 1
        self._record((self.sem[e], val, e), reads, writes)

    def dma(self, e, fn, reads, writes, semkey):
        self._deps(e, reads, writes, True)
        ent = self.dsem.get(semkey)
        if ent is None:
            ent = [self.nc.alloc_semaphore("d_" + str(len(self.dsem))), 0]
            self.dsem[semkey] = ent
        ins = fn(self.eng[e])
        ent[1] += 16
        ins.then_inc(ent[0], 16)
        self._record((ent[0], ent[1], None), reads, writes)

    def finalize_group(self, keys, semkey):
        ent = self.dsem[semkey]
        for k in keys:
            self.lastw[k] = (ent[0], ent[1], None)

    def barrier(self, exclude=()):
        ex = set(self.dsem[k][0].num for k in exclude if k in self.dsem)
        for e in self.eng:
            for s in self.eng:
                if s != e and self.cnt[s] > 0:
                    self._wait(e, (self.sem[s], self.cnt[s], s))
            for ent in self.dsem.values():
                if ent[1] > 0 and ent[0].num not in ex:
                    self._wait(e, (ent[0], ent[1], None))
        self.lastw = {k: t for k, t in self.lastw.items() if t[0].num in ex}
        self.readers = {}

    def wait_all_dma(self, e):
        for ent in self.dsem.values():
            if ent[1] > 0:
                self._wait(e, (ent[0], ent[1], None))


class Arena:
    def __init__(self, nc, base, limit, tag):
        self.nc, self.off, self.limit, self.tag, self.n = nc, base, limit, tag, 0

    def alloc(self, name, shape, dtype):
        isz = 4 if dtype in (F32, I32) else 2
        nbytes = isz
        for s in shape[1:]:
            nbytes *= s
        off = (self.off + 63) // 64 * 64
        assert off + nbytes <= self.limit, (self.tag, name, off, nbytes, self.limit)
        self.n += 1
        t = self.nc.alloc_sbuf_tensor_at(f"{self.tag}_{name}", list(shape), dtype, offset=off)
        self.off = off + nbytes
        return t


def build_program(S, stop=None):
    NB = S // TB
    NKT = S // 128
    nc = bass.Bass("TRN2", target_bir_lowering=False)
    dt_in = lambda n, shp, d=F32: nc.dram_tensor(n, shp, d, kind="ExternalInput").ap()
    x = dt_in("x", [S, D])
    pos = dt_in("pos", [S], I32)
    params = dt_in("params", [128, P_TOT])
    ident_d = dt_in("ident", [128, 128])
    perm_d = dt_in("perm", [128, 128])
    w_ada = dt_in("w_ada", [D, 6 * D])
    w_in = dt_in("w_in", [D, 6 * D])
    w_attn_o = dt_in("w_attn_o", [D, D])
    w_conv_o = dt_in("w_conv_o", [D, D])
    w_gate = dt_in("w_gate", [D, 2 * D])
    w_out = dt_in("w_out", [D, D])
    w_up = dt_in("w_up", [D, 2 * DFF])
    w_down = dt_in("w_down", [DFF, D])
    out = nc.dram_tensor("out", [S, D], F32, kind="ExternalOutput").ap()

    scr = lambda n, shp, d=BF16: nc.dram_tensor(n, shp, d, kind="Internal").ap()
    Hs = scr("Hs", [128, KC, S])
    As = scr("As", [128, KC, S])
    COSs = scr("COSs", [128, S], F32)
    SINs = scr("SINs", [128, S], F32)
    WQs = scr("WQs", [128, KC, D])
    WC1 = scr("WC1", [8, 128, KC, 384])
    WC2 = scr("WC2", [8, 128, KC, 512])
    WC3 = scr("WC3", [2, 128, KC, 512])
    WC4 = scr("WC4", [11, 128, KC, 512])
    WC5 = scr("WC5", [8, 128, FC, 128])
    WAs = scr("WAs", [4, 128, KC, D])

    sc = Sched(nc)
    ROPE_KEYS = ["t1", "t2"]
    T1CUR = [None, None]
    CVX = ("cvq", "cv1", "cv2", "cv3", "cv4", "cv5", "cva")
    BASE = (nc._sbuf_addr_for_side(None) + 63) // 64 * 64
    TOP = nc.SBUF_PARTITION_SIZE_BYTES
    G = Arena(nc, BASE, BASE + 6 * 1024, "g")
    KV0 = BASE + 6 * 1024
    KV1 = KV0 + 128 * 1024
    ps = nc.alloc_psum_tensor("ps", [128, 4096], F32)
    bank = lambda b: ps[:, b * 512:(b + 1) * 512]
    pk_ = lambda b: ("ps", b)

    def finish():
        sc.wait_all_dma("sp")
        for e in ("pe", "act", "dve", "pool"):
            if sc.cnt[e] > 0:
                sc._wait("sp", (sc.sem[e], sc.cnt[e], e))
        return nc

    prm = G.alloc("prm", [128, P_TOT], F32)
    ident = G.alloc("ident", [128, 128], F32)
    ones_f = G.alloc("ones_f", [128, 128], F32)
    ones_b = G.alloc("ones_b", [128, 128], BF16)
    perm_b = G.alloc("perm_b", [128, 128], BF16)
    cst = G.alloc("cst", [128, 8], F32)
    mod = G.alloc("mod", [128, 48], F32)
    gs1 = G.alloc("gs1", [128, 8], F32)
    gs2 = G.alloc("gs2", [128, 8], F32)
    lamt = G.alloc("lamt", [128, 4], F32)
    gsub = G.alloc("gsub", [128, 1], F32)
    csil = G.alloc("csil", [128, 8], BF16)
    ltmp = G.alloc("ltmp", [128, 128], F32)
    asave = G.alloc("asave", [128, FC, 2], F32)
    bsave = G.alloc("bsave", [128, FC, 1], F32)

    sc.dma("sp", lambda q: q.dma_start(out=prm[:], in_=params), [], ["prm"], "c0")
    sc.dma("sp", lambda q: q.dma_start(out=ident[:], in_=ident_d), [], ["ident"], "c1")
    sc.dma("pool", lambda q: q.dma_start(out=perm_b[:], in_=perm_d), [], ["perm_b"], "c2")

    PA = Arena(nc, KV0, KV1, "pa")
    WA = PA.alloc("WA", [128, KC, 2 * D], BF16)
    w_ada_v = w_ada.rearrange("(k p) n -> p k n", p=128)
    for g in range(2):
        sc.dma("pool", lambda q, g=g: q.dma_start(out=WA[:, :, g * D:(g + 1) * D], in_=w_ada_v[:, :, g * D:(g + 1) * D]),
               [], [("WA", g)], ("WA", g))

    RA = Arena(nc, KV1, TOP, "ra")
    WKV = RA.alloc("WKV", [128, KC, 2 * D], BF16)
    w_in_v = w_in.rearrange("(k p) n -> p k n", p=128)
    for g in range(2):
        sc.dma("pool", lambda q, g=g: q.dma_start(out=WKV[:, :, g * D:(g + 1) * D], in_=w_in_v[:, :, (1 + g) * D:(2 + g) * D]),
               [], [("WKV", g)], ("WKV", g))

    if stop == 'p2':
        return finish()
    sc.op("dve", lambda v: v.memset(cst[:, 0:1], EPS), [], ["cst"])
    sc.op("dve", lambda v: v.memset(cst[:, 1:2], SUBLN_EPS), [], ["cst"])
    sc.op("dve", lambda v: v.memset(cst[:, 2:3], -math.pi), [], ["cst"])
    sc.op("dve", lambda v: v.memset(cst[:, 3:4], 0.0), [], ["cst"])
    sc.op("dve", lambda v: v.memset(ones_f[:], 1.0), [], ["ones_f"])
    sc.op("dve", lambda v: v.memset(ones_b[:], 1.0), [], ["ones_b"])
    sc.op("dve", lambda v: v.memset(asave[:], 0.0), [], ["asave"])
    sc.op("dve", lambda v: v.memset(bsave[:], 0.0), [], ["bsave"])
    L0 = P_LAM
    sc.op("dve", lambda v: v.tensor_tensor(ltmp[:, 0:64], prm[:, L0:L0 + 64], prm[:, L0 + 64:L0 + 128], op=ALU.mult), ["prm"], ["ltmp"])
    sc.op("dve", lambda v: v.tensor_tensor(ltmp[:, 64:128], prm[:, L0 + 128:L0 + 192], prm[:, L0 + 192:L0 + 256], op=ALU.mult), ["prm"], ["ltmp"])
    sc.op("dve", lambda v: v.reduce_sum(lamt[:, 0:1], ltmp[:, 0:64], axis=mybir.AxisListType.X), ["ltmp"], ["lamt"])
    sc.op("dve", lambda v: v.reduce_sum(lamt[:, 1:2], ltmp[:, 64:128], axis=mybir.AxisListType.X), ["ltmp"], ["lamt"])
    sc.op("act", lambda a: a.activation(out=lamt[:, 0:2], in_=lamt[:, 0:2], func=AF.Exp), ["lamt"], ["lamt"])
    sc.op("dve", lambda v: v.tensor_tensor(lamt[:, 2:3], lamt[:, 0:1], lamt[:, 1:2], op=ALU.subtract), ["lamt"], ["lamt"])
    sc.op("dve", lambda v: v.tensor_scalar(lamt[:, 3:4], lamt[:, 2:3], LAMBDA_INIT, None, op0=ALU.add), ["lamt"], ["lamt"])
    sc.op("dve", lambda v: v.tensor_scalar(gsub[:], prm[:, P_SUBLN:P_SUBLN + 1], 1.0 - LAMBDA_INIT, None, op0=ALU.mult), ["prm"], ["gsub"])
    sc.op("act", lambda a: a.activation(out=csil[:], in_=prm[:, P_C:P_C + 8], func=AF.Silu), ["prm"], ["csil"])
    for j in range(16):
        for k in range(KC):
            sc.op("pe", lambda p, j=j, k=k: p.matmul(ps[:, j:j + 1], WA[:, k, j * 128:(j + 1) * 128], csil[:, k:k + 1],
                                                     start=(k == 0), stop=(k == KC - 1)),
                  [("WA", j // 8), "csil"], [pk_(0)], inc=(k == KC - 1))
    sc.op("dve", lambda v: v.tensor_tensor(mod[:, 0:16], ps[:, 0:16], prm[:, P_BADA:P_BADA + 16], op=ALU.add), [pk_(0), "prm"], ["mod"])
    sc.op("dve", lambda v: v.scalar_tensor_tensor(out=gs1[:], in0=mod[:, 8:16], scalar=1.0, in1=prm[:, P_N1G:P_N1G + 8], op0=ALU.add, op1=ALU.mult), ["mod", "prm"], ["gs1"])
    SH1 = lambda k: mod[:, k:k + 1]
    G1 = lambda k: mod[:, 16 + k:17 + k]
    SH2 = lambda k: mod[:, 24 + k:25 + k]
    G2 = lambda k: mod[:, 40 + k:41 + k]

    if stop == 'p3':
        return finish()
    TA = Arena(nc, RA.off, TOP, "ta")
    CW = min(1024, S)
    tpi = TA.alloc("tpi", [128, CW], I32)
    tr = TA.alloc("tr", [128, CW], F32)
    tr2 = TA.alloc("tr2", [128, CW], F32)
    tki = TA.alloc("tki", [128, CW], I32)
    tkf = TA.alloc("tkf", [128, CW], F32)
    tf_ = [TA.alloc("tf0", [128, CW], F32), TA.alloc("tf1", [128, CW], F32)]
    RP = P_ROPE
    for ci in range(S // CW):
        c0 = ci * CW
        sc.dma("sp", lambda q, c0=c0: q.dma_start(out=tpi[:], in_=pos[c0:c0 + CW].partition_broadcast(128)), [], ["tpi"], "tpi")
        sc.op("dve", lambda v: v.tensor_copy(tr[:], tpi[:]), ["tpi"], ["tr"])
        sc.op("dve", lambda v: v.tensor_scalar(tr[:], tr[:], prm[:, RP:RP + 1], 0.5, op0=ALU.mult, op1=ALU.add), ["tr", "prm"], ["tr"])
        for which in range(2):
            src = tr
            if which == 1:
                sc.op("dve", lambda v: v.tensor_scalar(tr2[:], tr[:], 0.25, None, op0=ALU.add), ["tr"], ["tr2"])
                src = tr2
            tf = tf_[which]
            sc.op("dve", lambda v, src=src: v.tensor_copy(tki[:], src[:]), ["tr", "tr2"], ["tki"])
            sc.op("dve", lambda v: v.tensor_copy(tkf[:], tki[:]), ["tki"], ["tkf"])
            sc.op("dve", lambda v, src=src, tf=tf: v.tensor_tensor(tf[:], src[:], tkf[:], op=ALU.subtract), ["tr", "tr2", "tkf"], [("tf", which)])
            sc.op("dve", lambda v, tf=tf: v.tensor_scalar(tkf[:], tf[:], 0.0, None, op0=ALU.is_lt), [("tf", which)], ["tkf"])
            sc.op("dve", lambda v, tf=tf: v.tensor_tensor(tf[:], tf[:], tkf[:], op=ALU.add), ["tkf"], [("tf", which)])
            sc.op("dve", lambda v, tf=tf: v.tensor_scalar(tf[:], tf[:], 0.0, 0.99999994, op0=ALU.max, op1=ALU.min), [], [("tf", which)])
            if which == 0:
                sc.op("act", lambda a, tf=tf: a.activation(out=tf[:], in_=tf[:], func=AF.Sin, bias=cst[:, 2:3], scale=2 * math.pi),
                      ["cst"], [("tf", which)])
                sc.op("dve", lambda v, tf=tf: v.tensor_scalar(tf[:], tf[:], prm[:, RP + 3:RP + 4], None, op0=ALU.mult), ["prm"], [("tf", which)])
                sc.dma("sp", lambda q, tf=tf, c0=c0: q.dma_start(out=SINs[:, c0:c0 + CW], in_=tf[:]), [("tf", which)], [("SINs", ci)], ("tabw", which))
            else:
                sc.op("act", lambda a, tf=tf: a.activation(out=tf[:], in_=tf[:], func=AF.Sin, bias=cst[:, 2:3], scale=2 * math.pi),
                      ["cst"], [("tf", which)])
                sc.dma("sp", lambda q, tf=tf, c0=c0: q.dma_start(out=COSs[:, c0:c0 + CW], in_=tf[:]), [("tf", which)], [("COSs", ci)], ("tabw", which))

    if stop == 'pro':
        return finish()
    sc.barrier(CVX)
    sc.dma("pool", lambda q: q.dma_start(out=WQs, in_=w_in_v[:, :, 0:D]), [], ["WQs"], "cvq")
    cv_keys = {1: [], 2: [], 3: [], 4: [], 5: []}
    for m in range(8):
        for s_, c0 in enumerate((3 * D, 4 * D, 5 * D)):
            key = ("WC1", m, s_)
            sc.dma("pool", lambda q, m=m, s_=s_, c0=c0: q.dma_start(
                out=WC1[m, :, :, s_ * 128:(s_ + 1) * 128], in_=w_in_v[:, :, c0 + m * 128:c0 + (m + 1) * 128]),
                [], [key], "cv1")
            cv_keys[1].append(key)
    wao_v = w_attn_o.rearrange("(k p) n -> p k n", p=128)
    wco_v = w_conv_o.rearrange("(k p) n -> p k n", p=128)
    wg_v = w_gate.rearrange("(k p) n -> p k n", p=128)
    for m in range(8):
        srcs = [wao_v[:, :, m * 128:(m + 1) * 128], wco_v[:, :, m * 128:(m + 1) * 128],
                wg_v[:, :, m * 128:(m + 1) * 128], wg_v[:, :, D + m * 128:D + (m + 1) * 128]]
        for s_, src in enumerate(srcs):
            key = ("WC2", m, s_)
            sc.dma("pool", lambda q, m=m, s_=s_, src=src: q.dma_start(out=WC2[m, :, :, s_ * 128:(s_ + 1) * 128], in_=src),
                   [], [key], "cv2")
            cv_keys[2].append(key)
    wout_v = w_out.rearrange("(k p) n -> p k n", p=128)
    for u in range(2):
        key = ("WC3", u)
        sc.dma("pool", lambda q, u=u: q.dma_start(out=WC3[u], in_=wout_v[:, :, u * 512:(u + 1) * 512]), [], [key], "cv3")
        cv_keys[3].append(key)
    wup_v = w_up.rearrange("(k p) n -> p k n", p=128)
    for u in range(11):
        for cc in range(2):
            for ab in range(2):
                c = 2 * u + cc
                key = ("WC4", u, cc, ab)
                o0 = (cc * 2 + ab) * 128
                sc.dma("pool", lambda q, u=u, o0=o0, ab=ab, c=c: q.dma_start(
                    out=WC4[u, :, :, o0:o0 + 128], in_=wup_v[:, :, ab * DFF + c * 128:ab * DFF + (c + 1) * 128]),
                    [], [key], "cv4")
                cv_keys[4].append(key)
    wd_v = w_down.rearrange("(c p) n -> p c n", p=128)
    for m in range(8):
        key = ("WC5", m)
        sc.dma("pool", lambda q, m=m: q.dma_start(out=WC5[m], in_=wd_v[:, :, m * 128:(m + 1) * 128]), [], [key], "cv5")
        cv_keys[5].append(key)
    for i_ in range(1, 6):
        sc.finalize_group(cv_keys[i_], "cv%d" % i_)
    wa_keys = []
    for g in range(4):
        key = ("WAs", g)
        sc.dma("pool", lambda q, g=g: q.dma_start(out=WAs[g], in_=w_ada_v[:, :, (g + 2) * D:(g + 3) * D]), [], [key], "cva")
        wa_keys.append(key)
    sc.finalize_group(wa_keys, "cva")


    KVA = Arena(nc, KV0, KV1, "kv")
    KT = KVA.alloc("KT", [128, KC, S], BF16)
    Vt = KVA.alloc("V", [128, NKT, D], BF16)
    xt = [RA.alloc("xt0", [128, D], F32), RA.alloc("xt1", [128, D], F32)]
    junk = RA.alloc("junk", [128, D], BF16)
    hT = [RA.alloc("hT0", [128, KC, TB], BF16), RA.alloc("hT1", [128, KC, TB], BF16)]
    cosb = RA.alloc("cosb", [128, TB], F32)
    sinb = RA.alloc("sinb", [128, TB], F32)
    kb = [RA.alloc("kb0", [128, TB], BF16), RA.alloc("kb1", [128, TB], BF16)]
    t1 = RA.alloc("t1", [128, TB], F32)
    t2 = RA.alloc("t2", [128, TB], F32)
    t1b = RA.alloc("t1b", [128, TB], F32)
    T1CUR[0], T1CUR[1] = t1, t2
    stat = RA.alloc("stat", [128, 8], F32)


    def rope_a(psrc_bank, kbi, copy_eng="act", t1o=None):
        k1, k2 = ROPE_KEYS
        kbt = kb[kbi]
        t1 = T1CUR[0]
        if t1o is not None:
            t1, k1 = t1o
        if copy_eng == "act":
            sc.op("act", lambda a: a.activation(out=kbt[:], in_=bank(psrc_bank), func=AF.Copy), [pk_(psrc_bank)], [("kb", kbi)])
        else:
            sc.op("dve", lambda v: v.tensor_copy(kbt[:], bank(psrc_bank)), [pk_(psrc_bank)], [("kb", kbi)])
        sc.op("dve", lambda v: v.tensor_tensor(t1[:], bank(psrc_bank), cosb[:], op=ALU.mult), [pk_(psrc_bank), "cosb"], [k1])

    def rope_b(pperm_bank, dst_ap, dst_key, kbi, t1o=None):
        k1, k2 = ROPE_KEYS
        kbt = kb[kbi]
        t1, t2 = T1CUR[0], T1CUR[1]
        if t1o is not None:
            t1, k1 = t1o
        sc.op("pe", lambda p: p.matmul(bank(pperm_bank), perm_b[:], kbt[:], start=True, stop=True), [("kb", kbi), "perm_b"], [pk_(pperm_bank)])
        sc.op("dve", lambda v: v.tensor_tensor(t2[:], bank(pperm_bank), sinb[:], op=ALU.mult), [pk_(pperm_bank), "sinb"], [k2])
        sc.op("dve", lambda v: v.tensor_tensor(dst_ap, t1[:], t2[:], op=ALU.add), [k1, k2], [dst_key])

    def rope_chunk(psrc_bank, pperm_bank, dst_ap, dst_key, kbi, copy_eng="act"):
        rope_a(psrc_bank, kbi, copy_eng)
        rope_b(pperm_bank, dst_ap, dst_key, kbi)

    def load_tables(t0):
        sc.dma("sp", lambda q: q.dma_start(out=cosb[:], in_=COSs[:, t0:t0 + TB]), [("COSs", t0 // 512)], ["cosb"], "cosb")
        sc.dma("sp", lambda q: q.dma_start(out=sinb[:], in_=SINs[:, t0:t0 + TB]), [("SINs", t0 // 512)], ["sinb"], "sinb")

    tile_ctr = [0]
    pend_rope = []

    def prep_steps(i):
        t0 = i * TB
        hb = i % 2
        hTb = hT[hb]
        steps = []
        for j in range(4):
            xs = tile_ctr[0] % 2
            tile_ctr[0] += 1

            def stepA0(j=j, xs=xs):
                xtt = xt[xs]
                r0 = t0 + j * 128
                sc.dma("act", lambda q: q.dma_start(out=xtt[:], in_=x[r0:r0 + 128, :]), [], [("xt", xs)], ("xt", xs))

            def stepA(j=j, xs=xs):
                xtt = xt[xs]
                sc.op("act", lambda a: a.memzero(stat[:, 0:1]), [], ["stat"])
                sc.op("act", lambda a: a.activation(out=junk[:], in_=xtt[:], func=AF.Square, accum_out=stat[:, 0:1]), [("xt", xs)], ["junk", "stat"])
                sc.op("act", lambda a: a.activation(out=stat[:, 1:2], in_=stat[:, 0:1], func=AF.Ln, scale=1.0 / D, bias=cst[:, 0:1]), ["stat", "cst"], ["stat"])
                sc.op("act", lambda a: a.activation(out=stat[:, 2:3], in_=stat[:, 1:2], func=AF.Exp, scale=-0.5), ["stat"], ["stat"])
                sc.op("act", lambda a: a.activation(out=xtt[:], in_=xtt[:], func=AF.Identity, scale=stat[:, 2:3]), ["stat"], [("xt", xs)])

            def stepB(j=j, xs=xs):
                xtt = xt[xs]
                for k in range(KC):
                    b_ = k // 4
                    sc.op("pe", lambda p, b_=b_, k=k: p.transpose(ps[:, b_ * 512 + (k % 4) * 128: b_ * 512 + (k % 4 + 1) * 128],
                                                                 xtt[:, k * 128:(k + 1) * 128], ident[:]),
                          [("xt", xs), "ident"], [pk_(b_)])
                for k in range(KC):
                    b_ = k // 4
                    if b_ == 0:
                        sc.op("dve", lambda v, b_=b_, k=k: v.tensor_scalar(
                            hTb[:, k, j * 128:(j + 1) * 128], ps[:, b_ * 512 + (k % 4) * 128: b_ * 512 + (k % 4 + 1) * 128],
                            gs1[:, k:k + 1], SH1(k), op0=ALU.mult, op1=ALU.add),
                            [pk_(b_), "gs1", "mod"], [("hT", hb, 0)])
                    else:
                        sc.op("act", lambda a, b_=b_, k=k: a.activation(
                            out=hTb[:, k, j * 128:(j + 1) * 128], in_=ps[:, b_ * 512 + (k % 4) * 128: b_ * 512 + (k % 4 + 1) * 128],
                            func=AF.Identity, scale=gs1[:, k:k + 1], bias=SH1(k)),
                            [pk_(b_), "gs1", "mod"], [("hT", hb, 1)])
                if j == 3:
                    sc.dma("sp", lambda q: q.dma_start(out=Hs[:, :, t0:t0 + TB], in_=hTb[:]), [("hT", hb, 0), ("hT", hb, 1)], [("Hs", i)], ("hsw", hb))
            steps.append((stepA0, stepA, stepB))
        return steps

    def kv_steps(i):
        t0 = i * TB
        hb = i % 2
        hTb = hT[hb]
        steps = []
        for m in range(KC):
            def kstep(m=m):
                if m == 0 and i == 0:
                    load_tables(t0)
                pb = 4 + (m % 2)
                for k in range(KC):
                    sc.op("pe", lambda p, k=k: p.matmul(bank(pb), WKV[:, k, m * 128:(m + 1) * 128], hTb[:, k, :],
                                                       start=(k == 0), stop=(k == KC - 1)),
                          [("WKV", 0), ("hT", hb, 0), ("hT", hb, 1)], [pk_(pb)], inc=(k == KC - 1))
                t1sel = [(t1, "t1"), (t1b, "t1b")]
                rope_a(pb, m % 2, "act", t1sel[m % 2])
                if m >= 1:
                    rope_b(6 + ((m - 1) % 2), KT[:, m - 1, t0:t0 + TB], "KT", (m - 1) % 2, t1sel[(m - 1) % 2])
                if m == KC - 1:
                    pend_rope.append(lambda: rope_b(6 + (m % 2), KT[:, m, t0:t0 + TB], "KT", m % 2, t1sel[m % 2]))
            steps.append(kstep)
        for j in range(4):
            for nh in range(2):
                def vstep(j=j, nh=nh):
                    while pend_rope:
                        pend_rope.pop()()
                    if j == 0 and nh == 0 and i + 1 < NB:
                        load_tables(t0 + TB)
                    pb = 2 + nh
                    for k in range(KC):
                        sc.op("pe", lambda p, k=k: p.matmul(bank(pb), hTb[:, k, j * 128:(j + 1) * 128],
                                                           WKV[:, k, D + nh * 512:D + (nh + 1) * 512],
                                                           start=(k == 0), stop=(k == KC - 1)),
                              [("WKV", 1), ("hT", hb, 0), ("hT", hb, 1)], [pk_(pb)], inc=(k == KC - 1))
                    sc.op("act", lambda a: a.activation(out=Vt[:, i * 4 + j, nh * 512:(nh + 1) * 512], in_=bank(pb), func=AF.Copy),
                          [pk_(pb)], ["V"])
                steps.append(vstep)
        return steps

    allprep = [st_ for i_ in range(NB) for st_ in prep_steps(i_)]
    NT = len(allprep)
    allprep[0][0]()
    for g_ in range(4):
        if g_ + 1 < NT:
            allprep[g_ + 1][0]()
        allprep[g_][1]()
        allprep[g_][2]()
    for i in range(NB):
        ks = kv_steps(i)
        ki = 0
        for pj in range(4):
            g_ = 4 * (i + 1) + pj
            if g_ + 1 < NT:
                allprep[g_ + 1][0]()
            if g_ < NT:
                allprep[g_][1]()
            for _ in range(4):
                ks[ki]()
                ki += 1
            if g_ < NT:
                allprep[g_][2]()

    if stop == 'A':
        return finish()
    sc.barrier(CVX)

    RB = Arena(nc, KV1, TOP, "rb")
    WQ = RB.alloc("WQ", [128, KC, D], BF16)
    hTB1 = RB.alloc("hT0", [128, KC, TB], BF16)
    QT2 = [RB.alloc("QT0", [128, KC, TB], BF16), RB.alloc("QT1", [128, KC, TB], BF16)]
    cosb = RB.alloc("cosb", [128, TB], F32)
    sinb = RB.alloc("sinb", [128, TB], F32)
    kb = [RB.alloc("kb0", [128, TB], BF16), RB.alloc("kb1", [128, TB], BF16)]
    Eb = [RB.alloc("E%d" % e_, [128, 2 * TB], BF16) for e_ in range(3)]
    R1 = RB.alloc("R1", [128, TB], F32)
    R2 = RB.alloc("R2", [128, TB], F32)
    RS = RB.alloc("RS", [128, TB], F32)
    t1, t2 = R2, RS
    T1CUR[0], T1CUR[1] = t1, t2
    ROPE_KEYS[0], ROPE_KEYS[1] = "R2", "RS"
    Thead = RB.alloc("Thead", [128, 2 * TB], F32)
    U1s = RB.alloc("U1s", [128, TB], F32)
    U2s = RB.alloc("U2s", [128, TB], F32)
    Tsum = [RB.alloc("Ts0", [128, 2 * TB], BF16), RB.alloc("Ts1", [128, 2 * TB], BF16)]
    Ast = [RB.alloc("Ast0", [128, TB], BF16), RB.alloc("Ast1", [128, TB], BF16)]
    O2b = RB.alloc("O2b", [128, TB], BF16)

    sc.dma("sp", lambda q: q.dma_start(out=WQ[:], in_=WQs), ["WQs"], ["WQ"], "wq")
    sctr = [0]
    resv = [None]

    def next_slot():
        if resv[0] is not None:
            return 1 - resv[0]
        sl = sctr[0] % 2
        sctr[0] += 1
        return sl

    def load_hT(i):
        t0_ = i * TB
        sc.dma("sp", lambda q: q.dma_start(out=hTB1[:], in_=Hs[:, :, t0_:t0_ + TB]), [("Hs", i)], ["hTB"], "hTB")
        load_tables(t0_)

    def q_mm(i, m):
        sl = next_slot()
        resv[0] = sl
        b0 = 2 * sl
        for k in range(KC):
            sc.op("pe", lambda p, k=k: p.matmul(bank(b0), WQ[:, k, m * 128:(m + 1) * 128], hTB1[:, k, :], start=(k == 0), stop=(k == KC - 1)),
                  ["WQ", "hTB"], [pk_(b0)], inc=(k == KC - 1))
        return b0

    def q_rope(i, m, b0):
        QTd = QT2[i % 2]
        rope_chunk(b0, b0 + 1, QTd[:, m, :], ("QT", i % 2, m), m % 2, copy_eng="dve")
        resv[0] = None

    def q_part(i, m, k0, k1_):
        for k in range(k0, k1_):
            sc.op("pe", lambda p, k=k: p.matmul(bank(6), WQ[:, k, m * 128:(m + 1) * 128], hTB1[:, k, :], start=(k == 0), stop=(k == KC - 1)),
                  ["WQ", "hTB"], [pk_(6)], inc=(k == k1_ - 1))

    def q_chunk(i, m):
        QTd = QT2[i % 2]
        q_part(i, m, 0, KC)
        rope_chunk(6, 7, QTd[:, m, :], ("QT", i % 2, m), m % 2, copy_eng="dve")

    load_hT(0)
    for m in range(KC):
        q_chunk(0, m)

    for i in range(NB):
        t0 = i * TB
        QT = QT2[i % 2]
        qk_ = lambda m: ("QT", i % 2, m)
        if i + 1 < NB:
            load_hT(i + 1)

        def qk_exp(m, kt):
            eb = (m * NKT + kt) % 3
            E = Eb[eb]
            sl = next_slot()
            b0 = 2 * sl
            sc.op("pe", lambda p: p.matmul(bank(b0), KT[0:64, m, kt * 128:(kt + 1) * 128], QT[0:64, m, :], start=True, stop=True),
                  ["KT", qk_(m)], [pk_(b0)], inc=False)
            sc.op("pe", lambda p: p.matmul(bank(b0 + 1), KT[64:128, m, kt * 128:(kt + 1) * 128], QT[64:128, m, :], start=True, stop=True),
                  ["KT", qk_(m)], [pk_(b0 + 1)])
            sc.op("act", lambda a: a.activation(out=E[:], in_=ps[:, b0 * 512:(b0 + 2) * 512], func=AF.Exp, scale=0.125), [pk_(b0), pk_(b0 + 1)], [("E", eb)])

        GS = min(8, NKT)
        NG = NKT // GS

        def pv(m, kt):
            eb = (m * NKT + kt) % 3
            E = Eb[eb]
            st, sp_ = (kt == 0), (kt == NKT - 1)
            sc.op("pe", lambda p: p.matmul(bank(4), Vt[:, kt, m * 128:(m + 1) * 128], E[:, 0:512], start=st, stop=sp_),
                  ["V", ("E", eb)], [pk_(4)], inc=False)
            sc.op("pe", lambda p: p.matmul(bank(5), Vt[:, kt, m * 128:(m + 1) * 128], E[:, 512:1024], start=st, stop=sp_),
                  ["V", ("E", eb)], [pk_(5)])

        def dsum(m, kt):
            eb = (m * NKT + kt) % 3
            E = Eb[eb]
            g, r = kt // GS, kt % GS
            gi = (m * NG + g) % 2
            T = Tsum[gi]
            if r == 0:
                sc.op("dve", lambda v: v.tensor_copy(T[:], E[:]), [("E", eb)], [("Ts", gi)])
            else:
                sc.op("dve", lambda v: v.tensor_tensor(T[:], T[:], E[:], op=ALU.add), [("E", eb)], [("Ts", gi)])
            if r == GS - 1:
                if g == 0:
                    sc.op("dve", lambda v: v.tensor_copy(Thead[:], T[:]), [("Ts", gi)], ["Thead"])
                else:
                    sc.op("dve", lambda v: v.tensor_tensor(Thead[:], Thead[:], T[:], op=ALU.add), [("Ts", gi)], ["Thead"])

        def pvs(m, kt):
            pv(m, kt)
            if kt == NKT - 1:
                sc.op("dve", lambda v: v.tensor_copy(U1s[:], bank(4)), [pk_(4)], ["U1s"])
                sc.op("dve", lambda v: v.tensor_copy(U2s[:], bank(5)), [pk_(5)], ["U2s"])
            dsum(m, kt)

        def epi1(m):
            pass

        def epi1a():
            sc.op("pe", lambda p: p.matmul(bank(6), ones_f[:], Thead[:, 0:512], start=True, stop=True), ["ones_f", "Thead"], [pk_(6)])
            sc.op("pe", lambda p: p.matmul(bank(7), ones_f[:], Thead[:, 512:1024], start=True, stop=True), ["ones_f", "Thead"], [pk_(7)])
            sc.op("act", lambda a: a.activation(out=R1[:], in_=bank(6), func=AF.Ln), [pk_(6)], ["R1"])
            sc.op("act", lambda a: a.activation(out=R2[:], in_=bank(7), func=AF.Ln), [pk_(7)], ["R2"])
            sc.op("act", lambda a: a.activation(out=R1[:], in_=R1[:], func=AF.Exp, scale=-1.0), [], ["R1"])
            sc.op("act", lambda a: a.activation(out=R2[:], in_=R2[:], func=AF.Exp, scale=-1.0), [], ["R2"])

        def epi1b():
            sc.op("dve", lambda v: v.tensor_tensor(U1s[:], U1s[:], R1[:], op=ALU.mult), ["R1"], ["U1s"])
            sc.op("dve", lambda v: v.scalar_tensor_tensor(out=U2s[:], in0=U2s[:], scalar=lamt[:, 3:4], in1=R2[:], op0=ALU.mult, op1=ALU.mult),
                  ["R2", "lamt"], ["U2s"])
            sc.op("pool", lambda g: g.tensor_tensor(R1[:], U1s[:], U2s[:], op=ALU.subtract), ["U1s", "U2s"], ["R1"])
            sc.op("pool", lambda g: g.tensor_tensor(O2b[:], R1[:], R1[:], op=ALU.mult), ["R1"], ["O2b"])

        def epi2(m):
            ab_ = m % 2
            pb = 6
            sc.op("pe", lambda p: p.matmul(bank(pb), ones_b[:], O2b[:], start=True, stop=True), ["ones_b", "O2b"], [pk_(pb)])
            sc.op("act", lambda a: a.activation(out=RS[:], in_=bank(pb), func=AF.Ln, scale=1.0 / 128, bias=cst[:, 1:2]), [pk_(pb), "cst"], ["RS"])
            sc.op("act", lambda a: a.activation(out=RS[:], in_=RS[:], func=AF.Exp, scale=-0.5), [], ["RS"])
            sc.op("dve", lambda v: v.scalar_tensor_tensor(out=Ast[ab_][:], in0=R1[:], scalar=gsub[:], in1=RS[:], op0=ALU.mult, op1=ALU.mult),
                  ["R1", "RS", "gsub"], [("Ast", ab_)])
            sc.dma("sp", lambda q: q.dma_start(out=As[:, m, t0:t0 + TB], in_=Ast[ab_][:]), [("Ast", ab_)], [("As", i, m)], ("asw", ab_))

        pending = None
        big = NKT >= 32
        H_E1A, H_E1B, H_E2 = (3, 6, 14) if big else (2, 3, 3)
        its = [(m, kt) for m in range(KC) for kt in range(NKT)]
        for idx, (m, kt) in enumerate(its):
            qk_exp(m, kt)
            if idx >= 2:
                pm, pkt = its[idx - 2]
                pvs(pm, pkt)
                if pkt == NKT - 1:
                    pending = pm
            if pending is not None:
                if kt == H_E1A:
                    epi1a()
                if kt == H_E1B:
                    epi1b()
                if kt == H_E2:
                    epi2(pending)
                    pending = None
            if i + 1 < NB:
                QTd = QT2[(i + 1) % 2]
                if big:
                    if 17 <= kt <= 20:
                        q_part(i + 1, m, 2 * (kt - 17), 2 * (kt - 17) + 2)
                    if kt == 21:
                        rope_a(6, m % 2, "dve")
                    if kt == 22:
                        rope_b(7, QTd[:, m, :], ("QT", (i + 1) % 2, m), m % 2)
                else:
                    if kt == 4:
                        q_part(i + 1, m, 0, KC)
                    if kt == 5:
                        rope_a(6, m % 2, "dve")
                    if kt == 6:
                        rope_b(7, QTd[:, m, :], ("QT", (i + 1) % 2, m), m % 2)
        pvs(KC - 1, NKT - 2)
        pvs(KC - 1, NKT - 1)
        epi1a()
        epi1b()
        epi2(KC - 1)

    if stop == 'B':
        return finish()
    sc.barrier(CVX)

    CA = Arena(nc, KV0, TOP, "c")
    NSLOT = 4
    wr = [CA.alloc("wr%d" % s_, [128, KC * 512], BF16) for s_ in range(NSLOT)]
    hTC = CA.alloc("hT", [128, KC, TB + 2], BF16)
    ATc = CA.alloc("AT", [128, KC, TB], BF16)
    xtok = CA.alloc("xtok", [128, 4, D], F32)
    WT = CA.alloc("WT", [128, KC, TB], BF16)
    MT = CA.alloc("MT", [128, KC, TB], BF16)
    X1 = CA.alloc("X1", [128, KC, TB + 1], F32)
    h2T = CA.alloc("h2T", [128, KC, TB], BF16)
    Gt = CA.alloc("G", [128, FC, TB], BF16)
    GBm_2 = [CA.alloc("GBm0", [128, TB], BF16), CA.alloc("GBm1", [128, TB], BF16)]
    GCs_2 = [CA.alloc("GCs0", [128, TB + 2], F32), CA.alloc("GCs1", [128, TB + 2], F32)]
    Zt_2 = [CA.alloc("Z0", [128, TB + 2], F32), CA.alloc("Z1", [128, TB + 2], F32)]
    Cv_2 = [CA.alloc("Cv0", [128, TB], F32), CA.alloc("Cv1", [128, TB], F32)]
    ga_2 = [CA.alloc("ga0", [128, TB], F32), CA.alloc("ga1", [128, TB], F32)]
    gb2_2 = [CA.alloc("gb20", [128, TB], F32), CA.alloc("gb21", [128, TB], F32)]
    tA_2 = [CA.alloc("tA0", [128, TB], F32), CA.alloc("tA1", [128, TB], F32)]
    tB__2 = [CA.alloc("tB0", [128, TB], F32), CA.alloc("tB1", [128, TB], F32)]
    SQ2 = [CA.alloc("SQ0", [128, TB], BF16), CA.alloc("SQ1", [128, TB], BF16)]
    ostg = CA.alloc("ostg", [128, 4, D], F32)
    RSc = CA.alloc("RSc", [128, TB], F32)
    Aa_2 = [CA.alloc("Aa0", [128, TB + 2], F32), CA.alloc("Aa1", [128, TB + 2], F32)]
    Bb_2 = [CA.alloc("Bb0", [128, TB + 1], F32), CA.alloc("Bb1", [128, TB + 1], F32)]
    Cc_2 = [CA.alloc("Cc0", [128, TB], F32), CA.alloc("Cc1", [128, TB], F32)]
    Sg_2 = [CA.alloc("Sg0", [128, TB], F32), CA.alloc("Sg1", [128, TB], F32)]
    x1s = CA.alloc("x1s", [128, KC, 1], F32)

    sc.op("dve", lambda v: v.memset(X1[:], 0.0), [], [("X1", m_) for m_ in range(KC)])

    wctr = [0]

    def wload(src_ap, src_keys, nbytes_cols, view):
        s_ = wctr[0] % NSLOT
        wctr[0] += 1
        dst = wr[s_][:, 0:nbytes_cols]
        sc.dma("sp", lambda q: q.dma_start(out=dst, in_=src_ap), src_keys, [("wr", s_)], ("wr", s_))
        return view(wr[s_]), ("wr", s_)

    v3 = lambda n: (lambda t: t[:, 0:KC * n].rearrange("p (k n) -> p k n", k=KC))

    def rsqrt_bcast(src_keys, sq_aps, w, scale, eps_ap, pbank):
        n = len(sq_aps)
        for q_, (ap_fn, key) in enumerate(sq_aps):
            SQ = SQ2[q_ % 2]
            sc.op("act", lambda a, ap_fn=ap_fn, SQ=SQ: a.activation(out=SQ[:, 0:w], in_=ap_fn(), func=AF.Square), [key], [("SQ", q_ % 2)])
            sc.op("pe", lambda p, q_=q_, SQ=SQ: p.matmul(ps[:, pbank * 512:pbank * 512 + w], ones_b[:], SQ[:, 0:w], start=(q_ == 0), stop=(q_ == n - 1)),
                  [("SQ", q_ % 2), "ones_b"], [pk_(pbank)])
        sc.op("act", lambda a: a.activation(out=RSc[:, 0:w], in_=ps[:, pbank * 512:pbank * 512 + w], func=AF.Ln, scale=scale, bias=eps_ap), [pk_(pbank), "cst"], ["RSc"])
        sc.op("act", lambda a: a.activation(out=RSc[:, 0:w], in_=RSc[:, 0:w], func=AF.Exp, scale=-0.5), [], ["RSc"])

    def ffn_and_out(i, w, has_new, tok_lo, hook=None):
        wstate = {}

        def bufs(c):
            return (Aa_2[c % 2], Bb_2[c % 2], Cc_2[c % 2], Sg_2[c % 2],
                    ("Aa", c % 2), ("Bb", c % 2), ("Cc", c % 2), ("Sg", c % 2))

        def up_mm(c):
            Aa, Bb, Cc, Sg, kA, kB, kC, kS = bufs(c)
            u, cc = c // 2, c % 2
            if has_new:
                if cc == 0:
                    wstate["w"] = wload(WC4[u], [k_ for k_ in cv_keys[4] if k_[1] == u], KC * 512, v3(512))
                wt, wk = wstate["w"]
                ub = 2 * (c % 2)
                for ab in range(2):
                    pb = ub + ab
                    o0 = (cc * 2 + ab) * 128
                    for k in range(KC):
                        sc.op("pe", lambda p, pb=pb, k=k, o0=o0: p.matmul(bank(pb), wt[:, k, o0:o0 + 128], h2T[:, k, :],
                                                                         start=(k == 0), stop=(k == KC - 1)),
                              [wk, ("h2T", k)], [pk_(pb)], inc=(k == KC - 1))
                sc.op("act", lambda a: a.activation(out=Aa[:, 2:TB + 2], in_=bank(ub), func=AF.Copy), [pk_(ub)], [kA])
                sc.op("act", lambda a: a.activation(out=Bb[:, 1:TB + 1], in_=bank(ub + 1), func=AF.Copy), [pk_(ub + 1)], [kB])
            else:
                sc.op("pool", lambda g: g.memset(Aa[:, 2:3], 0.0), [], [kA])
            sc.op("pool", lambda g: g.tensor_copy(Aa[:, 0:2], asave[:, c, :]), ["asave"], [kA])
            sc.op("pool", lambda g: g.tensor_copy(Bb[:, 0:1], bsave[:, c, :]), ["bsave"], [kB])
            if has_new:
                sc.op("pool", lambda g: g.tensor_copy(asave[:, c, :], Aa[:, TB:TB + 2]), [kA], ["asave"])
                sc.op("pool", lambda g: g.tensor_copy(bsave[:, c, :], Bb[:, TB:TB + 1]), [kB], ["bsave"])

        def post(c):
            Aa, Bb, Cc, Sg, kA, kB, kC, kS = bufs(c)
            FW = lambda tap: prm[:, P_FCONVW + tap * FC + c:P_FCONVW + tap * FC + c + 1]
            sc.op("dve", lambda v: v.tensor_scalar(Cc[:, 0:w], Aa[:, 0:w], FW(0), None, op0=ALU.mult), [kA, "prm"], [kC])
            sc.op("dve", lambda v: v.scalar_tensor_tensor(out=Cc[:, 0:w], in0=Aa[:, 1:w + 1], scalar=FW(1), in1=Cc[:, 0:w], op0=ALU.mult, op1=ALU.add), [kA], [kC])
            sc.op("dve", lambda v: v.scalar_tensor_tensor(out=Cc[:, 0:w], in0=Aa[:, 2:w + 2], scalar=FW(2), in1=Cc[:, 0:w], op0=ALU.mult, op1=ALU.add), [kA], [kC])
            sc.op("act", lambda a: a.activation(out=Sg[:, 0:w], in_=Cc[:, 0:w], func=AF.Silu, bias=prm[:, P_FCONVB + c:P_FCONVB + c + 1]), [kC, "prm"], [kS])
            sc.op("pool", lambda g: g.tensor_tensor(Gt[:, c, 0:w], Sg[:, 0:w], Bb[:, 0:w], op=ALU.mult), [kS, kB], [("G", c)])

        up_mm(0)
        for c in range(FC):
            if c + 1 < FC:
                up_mm(c + 1)
            post(c)
            if c == 2 and hook is not None:
                hook()
        def fstat_mm(m_):
            SQ = SQ2[m_ % 2]
            sc.op("pe", lambda p: p.matmul(ps[:, 6 * 512:6 * 512 + w], ones_b[:], SQ[:, 0:w], start=(m_ == 0), stop=(m_ == KC - 1)),
                  [("SQ", m_ % 2), "ones_b"], [pk_(6)])

        for m in range(KC):
            wt, wk = wload(WC5[m], [("WC5", m)], FC * 128, lambda t: t[:, 0:FC * 128].rearrange("p (c n) -> p c n", c=FC))
            pb = 4 + (m % 2)
            for c in range(FC):
                sc.op("pe", lambda p, pb=pb, c=c, wt=wt: p.matmul(ps[:, pb * 512:pb * 512 + w], wt[:, c, :], Gt[:, c, 0:w], start=(c == 0), stop=(c == FC - 1)),
                      [wk, ("G", c)], [pk_(pb)], inc=(c == FC - 1))
            if m >= 1:
                fstat_mm(m - 1)
            sc.op("dve", lambda v, pb=pb, m=m: v.scalar_tensor_tensor(out=X1[:, m, 0:w], in0=ps[:, pb * 512:pb * 512 + w], scalar=G2(m), in1=X1[:, m, 0:w],
                                                                       op0=ALU.mult, op1=ALU.add), [pk_(pb), "mod"], [("X1", m)])
            sc.op("act", lambda a, m=m: a.activation(out=SQ2[m % 2][:, 0:w], in_=X1[:, m, 0:w], func=AF.Square), [("X1", m)], [("SQ", m % 2)])
        fstat_mm(KC - 1)
        sc.op("act", lambda a: a.activation(out=RSc[:, 0:w], in_=ps[:, 6 * 512:6 * 512 + w], func=AF.Ln, scale=1.0 / D, bias=cst[:, 0:1]), [pk_(6), "cst"], ["RSc"])
        sc.op("act", lambda a: a.activation(out=RSc[:, 0:w], in_=RSc[:, 0:w], func=AF.Exp, scale=-0.5), [], ["RSc"])
        for m in range(KC):
            sc.op("dve", lambda v, m=m: v.scalar_tensor_tensor(out=X1[:, m, 0:w], in0=X1[:, m, 0:w], scalar=prm[:, P_FG + m:P_FG + m + 1], in1=RSc[:, 0:w],
                                                               op0=ALU.mult, op1=ALU.mult), ["RSc", "prm"], [("X1", m)])
        ntile = (w + 127) // 128
        for j in range(ntile):
            wj = min(128, w - j * 128)
            for half in range(2):
                pb = 2 * (j % 2) + half
                for mm in range(4):
                    m = half * 4 + mm
                    sc.op("pe", lambda p, pb=pb, mm=mm, m=m, j=j, wj=wj: p.transpose(ps[0:wj, pb * 512 + mm * 128:pb * 512 + (mm + 1) * 128],
                                                                                    X1[:, m, j * 128:j * 128 + wj], ident[:]),
                          [("X1", m), "ident"], [pk_(pb)])
                sc.op("act", lambda a, pb=pb, half=half, j=j, wj=wj: a.activation(out=ostg[0:wj, j, half * 512:(half + 1) * 512], in_=ps[0:wj, pb * 512:(pb + 1) * 512], func=AF.Copy),
                      [pk_(pb)], ["ostg"])
            tok0 = tok_lo + j * 128
            p0 = 0
            if tok0 < 0:
                p0 = -tok0
            if wj - p0 > 0:
                sc.dma("act", lambda q, j=j, p0=p0, wj=wj, tok0=tok0: q.dma_start(out=out[tok0 + p0:tok0 + wj, :], in_=ostg[p0:wj, j, :]),
                       ["ostg"], [("out", i, j)], "outw")

    def load_ha(i):
        t0 = i * TB
        lo = max(t0 - 1, 0)
        hi = min(t0 + TB + 1, S)
        c_lo = lo - (t0 - 1)
        sc.dma("sp", lambda q: q.dma_start(out=hTC[:, :, c_lo:c_lo + (hi - lo)], in_=Hs[:, :, lo:hi]),
               [("Hs", j_) for j_ in range(max(i - 1, 0), min(i + 2, NB))], ["hTC"], "hTC")
        sc.dma("sp", lambda q: q.dma_start(out=ATc[:], in_=As[:, :, t0:t0 + TB]), [("As", i, m) for m in range(KC)], ["ATc"], "ATc")
        if i == 0:
            sc.op("dve", lambda v: v.memset(hTC[:, :, 0:1], 0.0), [], ["hTC"])
        if i == NB - 1:
            sc.op("dve", lambda v: v.memset(hTC[:, :, TB + 1:TB + 2], 0.0), [], ["hTC"])

    def load_x(i):
        t0 = i * TB
        sc.dma("sp", lambda q: q.dma_start(out=xtok[:], in_=x[t0:t0 + TB, :].rearrange("(j p) d -> p j d", p=128)), [], ["xtok"], "xtokl")

    def c1(i):
        for m in range(KC):
            wt, wk = wload(WC1[m], [("WC1", m, s_) for s_ in range(3)], KC * 384, v3(384))
            cb = 4 * (m % 2)
            GBm, GCs, Zt, Cv = GBm_2[m % 2], GCs_2[m % 2], Zt_2[m % 2], Cv_2[m % 2]
            kG, kGC, kZ, kCv = ("GBm", m % 2), ("GCs", m % 2), ("Z", m % 2), ("Cv", m % 2)
            for s_ in range(3):
                pb = cb + s_
                for k in range(KC):
                    sc.op("pe", lambda p, pb=pb, k=k, s_=s_, wt=wt: p.matmul(bank(pb), wt[:, k, s_ * 128:(s_ + 1) * 128], hTC[:, k, 1:TB + 1],
                                                                            start=(k == 0), stop=(k == KC - 1)),
                          [wk, "hTC"], [pk_(pb)], inc=(k == KC - 1))
            for s_ in (1, 2):
                for k in range(KC):
                    sc.op("pe", lambda p, k=k, s_=s_, wt=wt: p.matmul(ps[:, (cb + 3) * 512 + (s_ - 1) * 2:(cb + 3) * 512 + (s_ - 1) * 2 + 2], wt[:, k, s_ * 128:(s_ + 1) * 128],
                                                                     hTC[:, k, 0:TB + 2:TB + 1], start=(k == 0), stop=(k == KC - 1)),
                          [wk, "hTC"], [pk_(cb + 3)], inc=(k == KC - 1))
            sc.op("act", lambda a: a.activation(out=GBm[:], in_=bank(cb), func=AF.Copy), [pk_(cb)], [kG])
            sc.op("act", lambda a: a.activation(out=GCs[:, 1:TB + 1], in_=bank(cb + 1), func=AF.Copy), [pk_(cb + 1)], [kGC])
            sc.op("act", lambda a: a.activation(out=GCs[:, 0:TB + 2:TB + 1], in_=ps[:, (cb + 3) * 512:(cb + 3) * 512 + 2], func=AF.Copy), [pk_(cb + 3)], [kGC])
            sc.op("dve", lambda v: v.tensor_tensor(Zt[:, 1:TB + 1], bank(cb + 2), GCs[:, 1:TB + 1], op=ALU.mult), [pk_(cb + 2), kGC], [kZ])
            sc.op("dve", lambda v: v.tensor_tensor(Zt[:, 0:TB + 2:TB + 1], ps[:, (cb + 3) * 512 + 2:(cb + 3) * 512 + 4], GCs[:, 0:TB + 2:TB + 1], op=ALU.mult), [pk_(cb + 3), kGC], [kZ])
            if i == 0:
                sc.op("dve", lambda v: v.memset(Zt[:, 0:1], 0.0), [], [kZ])
            if i == NB - 1:
                sc.op("dve", lambda v: v.memset(Zt[:, TB + 1:TB + 2], 0.0), [], [kZ])
            CW_ = lambda tap, m=m: prm[:, P_CONVW + tap * 8 + m:P_CONVW + tap * 8 + m + 1]
            sc.op("dve", lambda v, CW_=CW_: v.tensor_scalar(Cv[:], Zt[:, 0:TB], CW_(0), None, op0=ALU.mult), [kZ, "prm"], [kCv])
            sc.op("dve", lambda v, CW_=CW_: v.scalar_tensor_tensor(out=Cv[:], in0=Zt[:, 1:TB + 1], scalar=CW_(1), in1=Cv[:], op0=ALU.mult, op1=ALU.add), [kZ], [kCv])
            sc.op("dve", lambda v, CW_=CW_: v.scalar_tensor_tensor(out=Cv[:], in0=Zt[:, 2:TB + 2], scalar=CW_(2), in1=Cv[:], op0=ALU.mult, op1=ALU.add), [kZ], [kCv])
            sc.op("pool", lambda g, m=m: g.tensor_tensor(WT[:, m, :], GBm[:], Cv[:], op=ALU.mult), [kG, kCv], [("WT", m)])

    load_ha(0)
    load_x(0)
    for g2_ in range(4):
        for half in range(2):
            wt, wk = wload(WAs[g2_][:, :, half * 512:(half + 1) * 512], [("WAs", g2_)], KC * 512, v3(512))
            for jj in range(4):
                j = 16 + g2_ * 8 + half * 4 + jj
                for k in range(KC):
                    sc.op("pe", lambda p, j=j, k=k, jj=jj, wt=wt: p.matmul(ps[:, j:j + 1], wt[:, k, jj * 128:(jj + 1) * 128], csil[:, k:k + 1],
                                                                          start=(k == 0), stop=(k == KC - 1)),
                          [wk, "csil"], [pk_(0)], inc=(k == KC - 1))
    sc.op("dve", lambda v: v.tensor_tensor(mod[:, 16:48], ps[:, 16:48], prm[:, P_BADA + 16:P_BADA + 48], op=ALU.add), [pk_(0), "prm"], ["mod"])
    sc.op("dve", lambda v: v.scalar_tensor_tensor(out=gs2[:], in0=mod[:, 32:40], scalar=1.0, in1=prm[:, P_N2G:P_N2G + 8], op0=ALU.add, op1=ALU.mult), ["mod", "prm"], ["gs2"])
    c1(0)
    for i in range(NB):
        t0 = i * TB
        for m in range(KC):
            wt, wk = wload(WC2[m], [("WC2", m, s_) for s_ in range(4)], KC * 512, v3(512))
            rhs_of = [ATc, WT, hTC, hTC]
            cb = 4 * (m % 2)
            ga, gb2, tA, tB_ = ga_2[m % 2], gb2_2[m % 2], tA_2[m % 2], tB__2[m % 2]
            kga, kgb, ktA, ktB = ("ga", m % 2), ("gb2", m % 2), ("tA", m % 2), ("tB", m % 2)
            for s_ in range(4):
                pb = cb + s_
                for k in range(KC):
                    rhs = rhs_of[s_][:, k, :] if s_ < 2 else hTC[:, k, 1:TB + 1]
                    sc.op("pe", lambda p, pb=pb, k=k, s_=s_, wt=wt, rhs=rhs: p.matmul(bank(pb), wt[:, k, s_ * 128:(s_ + 1) * 128], rhs,
                                                                                     start=(k == 0), stop=(k == KC - 1)),
                          [wk, "ATc", ("WT", k), "hTC"], [pk_(pb)], inc=(k == KC - 1))
            sc.op("act", lambda a, m=m: a.activation(out=ga[:], in_=bank(cb + 2), func=AF.Sigmoid, bias=prm[:, P_BGATE + m:P_BGATE + m + 1]), [pk_(cb + 2), "prm"], [kga])
            sc.op("act", lambda a, m=m: a.activation(out=gb2[:], in_=bank(cb + 3), func=AF.Sigmoid, bias=prm[:, P_BGATE + 8 + m:P_BGATE + 8 + m + 1]), [pk_(cb + 3), "prm"], [kgb])
            sc.op("dve", lambda v: v.tensor_tensor(tA[:], bank(cb), ga[:], op=ALU.mult), [pk_(cb), kga], [ktA])
            sc.op("dve", lambda v: v.tensor_tensor(tB_[:], bank(cb + 1), gb2[:], op=ALU.mult), [pk_(cb + 1), kgb], [ktB])
            sc.op("pool", lambda g, m=m: g.tensor_tensor(MT[:, m, :], tA[:], tB_[:], op=ALU.add), [ktA, ktB], [("MT", m)])
        wc3 = [wload(WC3[u_], [("WC3", u_)], KC * 512, v3(512)) for u_ in range(2)]
        if i + 1 < NB:
            load_ha(i + 1)
        def stat_mm(m_):
            SQ = SQ2[m_ % 2]
            sc.op("pe", lambda p: p.matmul(bank(6), ones_b[:], SQ[:], start=(m_ == 0), stop=(m_ == KC - 1)),
                  [("SQ", m_ % 2), "ones_b"], [pk_(6)])

        for u in range(2):
            wt, wk = wc3[u]
            for mm in range(4):
                m = u * 4 + mm
                pb = mm % 2
                for k in range(KC):
                    sc.op("pe", lambda p, pb=pb, k=k, mm=mm, wt=wt: p.matmul(bank(pb), wt[:, k, mm * 128:(mm + 1) * 128], MT[:, k, :],
                                                                            start=(k == 0), stop=(k == KC - 1)),
                          [wk, ("MT", k)], [pk_(pb)], inc=(k == KC - 1))
                pt = 2 + (mm % 2)
                for j in range(4):
                    sc.op("pe", lambda p, pt=pt, j=j, m=m: p.transpose(ps[:, pt * 512 + j * 128:pt * 512 + (j + 1) * 128], xtok[:, j, m * 128:(m + 1) * 128], ident[:]),
                          ["xtok", "ident"], [pk_(pt)])
                sc.op("act", lambda a, pt=pt, m=m: a.activation(out=X1[:, m, 1:TB + 1], in_=bank(pt), func=AF.Copy), [pk_(pt)], [("X1", m)])
                if m >= 1:
                    stat_mm(m - 1)
                sc.op("dve", lambda v, pb=pb, m=m: v.scalar_tensor_tensor(out=X1[:, m, 1:TB + 1], in0=bank(pb), scalar=G1(m), in1=X1[:, m, 1:TB + 1],
                                                                           op0=ALU.mult, op1=ALU.add), [pk_(pb), "mod"], [("X1", m)])
                sc.op("act", lambda a, m=m: a.activation(out=SQ2[m % 2][:], in_=X1[:, m, 1:TB + 1], func=AF.Square), [("X1", m)], [("SQ", m % 2)])
        stat_mm(KC - 1)
        sc.op("pool", lambda g: g.tensor_copy(x1s[:], X1[:, :, TB:TB + 1]), [("X1", m_) for m_ in range(KC)], ["x1s"])
        sc.op("act", lambda a: a.activation(out=RSc[:], in_=bank(6), func=AF.Ln, scale=1.0 / D, bias=cst[:, 0:1]), [pk_(6), "cst"], ["RSc"])
        sc.op("act", lambda a: a.activation(out=RSc[:], in_=RSc[:], func=AF.Exp, scale=-0.5), [], ["RSc"])
        for m in range(KC):
            tA, ktA = tA_2[m % 2], ("tA", m % 2)
            sc.op("dve", lambda v, m=m: v.scalar_tensor_tensor(out=tA[:], in0=X1[:, m, 1:TB + 1], scalar=gs2[:, m:m + 1], in1=RSc[:], op0=ALU.mult, op1=ALU.mult),
                  [("X1", m), "RSc", "gs2"], [ktA])
            sc.op("act", lambda a, m=m: a.activation(out=h2T[:, m, :], in_=tA[:], func=AF.Identity, bias=SH2(m)), [ktA, "mod"], [("h2T", m)])
        if i + 1 < NB:
            c1(i + 1)
        ffn_and_out(i, TB, True, t0 - 1, hook=(lambda i=i: load_x(i + 1)) if i + 1 < NB else None)
        sc.op("pool", lambda g: g.tensor_copy(X1[:, :, 0:1], x1s[:]), ["x1s"], [("X1", m_) for m_ in range(KC)])
    ffn_and_out(NB, 1, False, S - 1)

    return finish()


_PROG_CACHE = {}


def _host_consts():
    ident = np.eye(128, dtype=np.float32)
    perm = np.zeros((128, 128), dtype=np.float32)
    for m in range(128):
        partner = m + 32 if (m % 64) < 32 else m - 32
        perm[partner, m] = 1.0
    inv_freq = (10000.0 ** (-np.arange(0, 64, 2, dtype=np.float32) / np.float32(64))).astype(np.float32)
    rope = np.zeros((128, 4), dtype=np.float32)
    for p in range(128):
        s = -1.0 if (p % 64) < 32 else 1.0
        rope[p, 0] = np.float32(np.float64(inv_freq[p % 32]) / (2 * math.pi))
        rope[p, 1] = s * 2 * math.pi
        rope[p, 2] = -s * math.pi
        rope[p, 3] = s
    return ident, perm, rope


def _colmajor(v, nchunk):
    return np.ascontiguousarray(np.asarray(v, dtype=np.float32).reshape(nchunk, 128).T)


def run(inputs, S, stop=None):
    inputs = {k: np.asarray(v) for k, v in inputs.items()}
    B = inputs["x"].shape[0]
    if (S, stop) not in _PROG_CACHE:
        _PROG_CACHE[(S, stop)] = build_program(S, stop)
    nc = _PROG_CACHE[(S, stop)]
    ident, perm, rope = _host_consts()
    in_maps = []
    shared = {
        "ident": ident, "perm": perm,
        "w_ada": np.ascontiguousarray(inputs["w_ada"][0], dtype=np.float32),
        "w_in": np.ascontiguousarray(inputs["w_in"][0], dtype=np.float32),
        "w_attn_o": np.ascontiguousarray(inputs["w_attn_o"][0], dtype=np.float32),
        "w_conv_o": np.ascontiguousarray(inputs["w_conv_o"][0], dtype=np.float32),
        "w_gate": np.ascontiguousarray(inputs["w_gate"][0], dtype=np.float32),
        "w_out": np.ascontiguousarray(inputs["w_out"][0], dtype=np.float32),
        "w_up": np.ascontiguousarray(inputs["w_up"][0], dtype=np.float32),
        "w_down": np.ascontiguousarray(inputs["w_down"][0], dtype=np.float32),
    }
    lamrow = np.concatenate([inputs["lambda_q1"][0], inputs["lambda_k1"][0], inputs["lambda_q2"][0], inputs["lambda_k2"][0]]).astype(np.float32)
    for b in range(B):
        prm = np.zeros((128, P_TOT), dtype=np.float32)
        prm[:, P_BADA:P_BADA + 48] = _colmajor(inputs["b_ada"][0], 48)
        prm[:, P_N1G:P_N1G + 8] = _colmajor(inputs["norm1_g"][0], 8)
        prm[:, P_N2G:P_N2G + 8] = _colmajor(inputs["norm2_g"][0], 8)
        prm[:, P_FG:P_FG + 8] = _colmajor(inputs["final_g"], 8)
        for tap in range(3):
            prm[:, P_CONVW + tap * 8:P_CONVW + (tap + 1) * 8] = _colmajor(inputs["conv_w"][0, tap], 8)
            prm[:, P_FCONVW + tap * FC:P_FCONVW + (tap + 1) * FC] = _colmajor(inputs["ffn_conv_w"][0, tap], FC)
        prm[:, P_FCONVB:P_FCONVB + FC] = _colmajor(inputs["ffn_conv_b"][0], FC)
        prm[:, P_BGATE:P_BGATE + 16] = _colmajor(inputs["b_gate"][0], 16)
        prm[:, P_SUBLN:P_SUBLN + 1] = np.asarray(inputs["subln_g"][0], dtype=np.float32).reshape(128, 1)
        prm[:, P_LAM:P_LAM + 256] = np.broadcast_to(lamrow[None, :], (128, 256))
        prm[:, P_ROPE:P_ROPE + 4] = rope
        prm[:, P_C:P_C + 8] = _colmajor(inputs["c"][b], 8)
        d = dict(shared)
        d["x"] = np.ascontiguousarray(inputs["x"][b, :S], dtype=np.float32)
        d["pos"] = np.ascontiguousarray(inputs["positions"][b, :S], dtype=np.int32)
        d["params"] = prm
        in_maps.append(d)
    res = run_bass_kernel_spmd(nc, in_maps, core_ids=list(range(B)))
    return np.stack([np.asarray(r["out"], dtype=np.float32) for r in res.results], axis=0)


def kernel(**inputs):
    return run(inputs, 4096)
```

```python
import math
import numpy as np
import concourse.bass as bass
import concourse.mybir as mybir
from concourse.bass_utils import run_bass_kernel_spmd

F32 = mybir.dt.float32
BF16 = mybir.dt.bfloat16
I32 = mybir.dt.int32
AF = mybir.ActivationFunctionType
ALU = mybir.AluOpType

D = 1024
KC = 8
DFF = 2816
FC = 22
EPS = 1e-6
SUBLN_EPS = 1e-5
LAMBDA_INIT = 0.8 - 0.6 * math.exp(-0.3 * 0)
TB = 512

P_BADA = 0
P_N1G = P_BADA + 48
P_N2G = P_N1G + 8
P_FG = P_N2G + 8
P_CONVW = P_FG + 8
P_FCONVW = P_CONVW + 24
P_FCONVB = P_FCONVW + 66
P_BGATE = P_FCONVB + 22
P_SUBLN = P_BGATE + 16
P_LAM = P_SUBLN + 1
P_ROPE = P_LAM + 256
P_C = P_ROPE + 4
P_TOT = P_C + 8


class Sched:
    def __init__(self, nc):
        self.nc = nc
        self.eng = {"pe": nc.tensor, "act": nc.scalar, "dve": nc.vector, "pool": nc.gpsimd, "sp": nc.sync}
        self.sem = {k: nc.alloc_semaphore("sem_" + k) for k in self.eng}
        self.cnt = {k: 0 for k in self.eng}
        self.waited = {}
        self.lastw = {}
        self.readers = {}
        self.dsem = {}

    def _wait(self, e, tok, is_dma=False):
        sem, val, src = tok
        if src == e and not is_dma and e == "pe":
            return
        k = (e, sem.num)
        if self.waited.get(k, 0) >= val:
            return
        self.eng[e].wait_ge(sem, val)
        self.waited[k] = val

    def _deps(self, e, reads, writes, is_dma=False):
        for r in reads:
            t = self.lastw.get(r)
            if t is not None:
                self._wait(e, t, is_dma)
        for w in writes:
            t = self.lastw.get(w)
            if t is not None:
                self._wait(e, t, is_dma)
            for t in self.readers.get(w, {}).values():
                self._wait(e, t, is_dma)

    def _record(self, tok, reads, writes):
        for w in writes:
            self.lastw[w] = tok
            self.readers[w] = {}
        for r in reads:
            d = self.readers.setdefault(r, {})
            k = tok[0].num
            if k not in d or d[k][1] < tok[1]:
                d[k] = tok

    def op(self, e, fn, reads=(), writes=(), inc=True):
        pr = [r for r in reads if isinstance(r, tuple) and r[0] == "ps"]
        if pr:
            reads = [r for r in reads if not (isinstance(r, tuple) and r[0] == "ps")]
            writes = list(writes) + pr
        self._deps(e, reads, writes)
        ins = fn(self.eng[e])
        if inc:
            ins.then_inc(self.sem[e], 1)
            self.cnt[e] += 1
            val = self.cnt[e]
        else:
            val = self.cnt[e] + 1
        self._record((self.sem[e], val, e), reads, writes)

    def dma(self, e, fn, reads, writes, semkey):
        self._deps(e, reads, writes, True)
        ent = self.dsem.get(semkey)
        if ent is None:
            ent = [self.nc.alloc_semaphore("d_" + str(len(self.dsem))), 0]
            self.dsem[semkey] = ent
        ins = fn(self.eng[e])
        ent[1] += 16
        ins.then_inc(ent[0], 16)
        self._record((ent[0], ent[1], None), reads, writes)

    def finalize_group(self, keys, semkey):
        ent = self.dsem[semkey]
        for k in keys:
            self.lastw[k] = (ent[0], ent[1], None)

    def barrier(self, exclude=()):
        ex = set(self.dsem[k][0].num for k in exclude if k in self.dsem)
        for e in self.eng:
            for s in self.eng:
                if s != e and self.cnt[s] > 0:
                    self._wait(e, (self.sem[s], self.cnt[s], s))
            for ent in self.dsem.values():
                if ent[1] > 0 and ent[0].num not in ex:
                    self._wait(e, (ent[0], ent[1], None))
        self.lastw = {k: t for k, t in self.lastw.items() if t[0].num in ex}
        self.readers = {}

    def wait_all_dma(self, e):
        for ent in self.dsem.values():
            if ent[1] > 0:
                self._wait(e, (ent[0], ent[1], None))


class Arena:
    def __init__(self, nc, base, limit, tag):
        self.nc, self.off, self.limit, self.tag, self.n = nc, base, limit, tag, 0

    def alloc(self, name, shape, dtype):
        isz = 4 if dtype in (F32, I32) else 2
        nbytes = isz
        for s in shape[1:]:
            nbytes *= s
        off = (self.off + 63) // 64 * 64
        assert off + nbytes <= self.limit, (self.tag, name, off, nbytes, self.limit)
        self.n += 1
        t = self.nc.alloc_sbuf_tensor_at(f"{self.tag}_{name}", list(shape), dtype, offset=off)
        self.off = off + nbytes
        return t


def build_program(S, stop=None):
    NB = S // TB
    NKT = S // 128
    nc = bass.Bass("TRN2", target_bir_lowering=False)
    dt_in = lambda n, shp, d=F32: nc.dram_tensor(n, shp, d, kind="ExternalInput").ap()
    x = dt_in("x", [S, D])
    pos = dt_in("pos", [S], I32)
    params = dt_in("params", [128, P_TOT])
    ident_d = dt_in("ident", [128, 128])
    perm_d = dt_in("perm", [128, 128])
    w_ada = dt_in("w_ada", [D, 6 * D])
    w_in = dt_in("w_in", [D, 6 * D])
    w_attn_o = dt_in("w_attn_o", [D, D])
    w_conv_o = dt_in("w_conv_o", [D, D])
    w_gate = dt_in("w_gate", [D, 2 * D])
    w_out = dt_in("w_out", [D, D])
    w_up = dt_in("w_up", [D, 2 * DFF])
    w_down = dt_in("w_down", [DFF, D])
    out = nc.dram_tensor("out", [S, D], F32, kind="ExternalOutput").ap()

    scr = lambda n, shp, d=BF16: nc.dram_tensor(n, shp, d, kind="Internal").ap()
    Hs = scr("Hs", [128, KC, S])
    As = scr("As", [128, KC, S])
    COSs = scr("COSs", [128, S], F32)
    SINs = scr("SINs", [128, S], F32)
    WQs = scr("WQs", [128, KC, D])
    WC1 = scr("WC1", [8, 128, KC, 384])
    WC2 = scr("WC2", [8, 128, KC, 512])
    WC3 = scr("WC3", [2, 128, KC, 512])
    WC4 = scr("WC4", [11, 128, KC, 512])
    WC5 = scr("WC5", [8, 128, FC, 128])
    WAs = scr("WAs", [4, 128, KC, D])

    sc = Sched(nc)
    ROPE_KEYS = ["t1", "t2"]
    T1CUR = [None, None]
    CVX = ("cvq", "cv1", "cv2", "cv3", "cv4", "cv5", "cva")
    BASE = (nc._sbuf_addr_for_side(None) + 63) // 64 * 64
    TOP = nc.SBUF_PARTITION_SIZE_BYTES
    G = Arena(nc, BASE, BASE + 6 * 1024, "g")
    KV0 = BASE + 6 * 1024
    KV1 = KV0 + 128 * 1024
    ps = nc.alloc_psum_tensor("ps", [128, 4096], F32)
    bank = lambda b: ps[:, b * 512:(b + 1) * 512]
    pk_ = lambda b: ("ps", b)

    def finish():
        sc.wait_all_dma("sp")
        for e in ("pe", "act", "dve", "pool"):
            if sc.cnt[e] > 0:
                sc._wait("sp", (sc.sem[e], sc.cnt[e], e))
        return nc

    prm = G.alloc("prm", [128, P_TOT], F32)
    ident = G.alloc("ident", [128, 128], F32)
    ones_f = G.alloc("ones_f", [128, 128], F32)
    ones_b = G.alloc("ones_b", [128, 128], BF16)
    perm_b = G.alloc("perm_b", [128, 128], BF16)
    cst = G.alloc("cst", [128, 8], F32)
    mod = G.alloc("mod", [128, 48], F32)
    gs1 = G.alloc("gs1", [128, 8], F32)
    gs2 = G.alloc("gs2", [128, 8], F32)
    lamt = G.alloc("lamt", [128, 4], F32)
    gsub = G.alloc("gsub", [128, 1], F32)
    csil = G.alloc("csil", [128, 8], BF16)
    ltmp = G.alloc("ltmp", [128, 128], F32)
    asave = G.alloc("asave", [128, FC, 2], F32)
    bsave = G.alloc("bsave", [128, FC, 1], F32)

    sc.dma("sp", lambda q: q.dma_start(out=prm[:], in_=params), [], ["prm"], "c0")
    sc.dma("sp", lambda q: q.dma_start(out=ident[:], in_=ident_d), [], ["ident"], "c1")
    sc.dma("pool", lambda q: q.dma_start(out=perm_b[:], in_=perm_d), [], ["perm_b"], "c2")

    PA = Arena(nc, KV0, KV1, "pa")
    WA = PA.alloc("WA", [128, KC, 2 * D], BF16)
    w_ada_v = w_ada.rearrange("(k p) n -> p k n", p=128)
    for g in range(2):
        sc.dma("pool", lambda q, g=g: q.dma_start(out=WA[:, :, g * D:(g + 1) * D], in_=w_ada_v[:, :, g * D:(g + 1) * D]),
               [], [("WA", g)], ("WA", g))

    RA = Arena(nc, KV1, TOP, "ra")
    WKV = RA.alloc("WKV", [128, KC, 2 * D], BF16)
    w_in_v = w_in.rearrange("(k p) n -> p k n", p=128)
    for g in range(2):
        sc.dma("pool", lambda q, g=g: q.dma_start(out=WKV[:, :, g * D:(g + 1) * D], in_=w_in_v[:, :, (1 + g) * D:(2 + g) * D]),
               [], [("WKV", g)], ("WKV", g))

    if stop == 'p2':
        return finish()
    sc.op("dve", lambda v: v.memset(cst[:, 0:1], EPS), [], ["cst"])
    sc.op("dve", lambda v: v.memset(cst[:, 1:2], SUBLN_EPS), [], ["cst"])
    sc.op("dve", lambda v: v.memset(cst[:, 2:3], -math.pi), [], ["cst"])
    sc.op("dve", lambda v: v.memset(cst[:, 3:4], 0.0), [], ["cst"])
    sc.op("dve", lambda v: v.memset(ones_f[:], 1.0), [], ["ones_f"])
    sc.op("dve", lambda v: v.memset(ones_b[:], 1.0), [], ["ones_b"])
    sc.op("dve", lambda v: v.memset(asave[:], 0.0), [], ["asave"])
    sc.op("dve", lambda v: v.memset(bsave[:], 0.0), [], ["bsave"])
    L0 = P_LAM
    sc.op("dve", lambda v: v.tensor_tensor(ltmp[:, 0:64], prm[:, L0:L0 + 64], prm[:, L0 + 64:L0 + 128], op=ALU.mult), ["prm"], ["ltmp"])
    sc.op("dve", lambda v: v.tensor_tensor(ltmp[:, 64:128], prm[:, L0 + 128:L0 + 192], prm[:, L0 + 192:L0 + 256], op=ALU.mult), ["prm"], ["ltmp"])
    sc.op("dve", lambda v: v.reduce_sum(lamt[:, 0:1], ltmp[:, 0:64], axis=mybir.AxisListType.X), ["ltmp"], ["lamt"])
    sc.op("dve", lambda v: v.reduce_sum(lamt[:, 1:2], ltmp[:, 64:128], axis=mybir.AxisListType.X), ["ltmp"], ["lamt"])
    sc.op("act", lambda a: a.activation(out=lamt[:, 0:2], in_=lamt[:, 0:2], func=AF.Exp), ["lamt"], ["lamt"])
    sc.op("dve", lambda v: v.tensor_tensor(lamt[:, 2:3], lamt[:, 0:1], lamt[:, 1:2], op=ALU.subtract), ["lamt"], ["lamt"])
    sc.op("dve", lambda v: v.tensor_scalar(lamt[:, 3:4], lamt[:, 2:3], LAMBDA_INIT, None, op0=ALU.add), ["lamt"], ["lamt"])
    sc.op("dve", lambda v: v.tensor_scalar(gsub[:], prm[:, P_SUBLN:P_SUBLN + 1], 1.0 - LAMBDA_INIT, None, op0=ALU.mult), ["prm"], ["gsub"])
    sc.op("act", lambda a: a.activation(out=csil[:], in_=prm[:, P_C:P_C + 8], func=AF.Silu), ["prm"], ["csil"])
    for j in range(16):
        for k in range(KC):
            sc.op("pe", lambda p, j=j, k=k: p.matmul(ps[:, j:j + 1], WA[:, k, j * 128:(j + 1) * 128], csil[:, k:k + 1],
                                                     start=(k == 0), stop=(k == KC - 1)),
                  [("WA", j // 8), "csil"], [pk_(0)], inc=(k == KC - 1))
    sc.op("dve", lambda v: v.tensor_tensor(mod[:, 0:16], ps[:, 0:16], prm[:, P_BADA:P_BADA + 16], op=ALU.add), [pk_(0), "prm"], ["mod"])
    sc.op("dve", lambda v: v.scalar_tensor_tensor(out=gs1[:], in0=mod[:, 8:16], scalar=1.0, in1=prm[:, P_N1G:P_N1G + 8], op0=ALU.add, op1=ALU.mult), ["mod", "prm"], ["gs1"])
    SH1 = lambda k: mod[:, k:k + 1]
    G1 = lambda k: mod[:, 16 + k:17 + k]
    SH2 = lambda k: mod[:, 24 + k:25 + k]
    G2 = lambda k: mod[:, 40 + k:41 + k]

    if stop == 'p3':
        return finish()
    TA = Arena(nc, RA.off, TOP, "ta")
    CW = min(1024, S)
    tpi = TA.alloc("tpi", [128, CW], I32)
    tr = TA.alloc("tr", [128, CW], F32)
    tr2 = TA.alloc("tr2", [128, CW], F32)
    tki = TA.alloc("tki", [128, CW], I32)
    tkf = TA.alloc("tkf", [128, CW], F32)
    tf_ = [TA.alloc("tf0", [128, CW], F32), TA.alloc("tf1", [128, CW], F32)]
    RP = P_ROPE
    for ci in range(S // CW):
        c0 = ci * CW
        sc.dma("sp", lambda q, c0=c0: q.dma_start(out=tpi[:], in_=pos[c0:c0 + CW].partition_broadcast(128)), [], ["tpi"], "tpi")
        sc.op("dve", lambda v: v.tensor_copy(tr[:], tpi[:]), ["tpi"], ["tr"])
        sc.op("dve", lambda v: v.tensor_scalar(tr[:], tr[:], prm[:, RP:RP + 1], 0.5, op0=ALU.mult, op1=ALU.add), ["tr", "prm"], ["tr"])
        for which in range(2):
            src = tr
            if which == 1:
                sc.op("dve", lambda v: v.tensor_scalar(tr2[:], tr[:], 0.25, None, op0=ALU.add), ["tr"], ["tr2"])
                src = tr2
            tf = tf_[which]
            sc.op("dve", lambda v, src=src: v.tensor_copy(tki[:], src[:]), ["tr", "tr2"], ["tki"])
            sc.op("dve", lambda v: v.tensor_copy(tkf[:], tki[:]), ["tki"], ["tkf"])
            sc.op("dve", lambda v, src=src, tf=tf: v.tensor_tensor(tf[:], src[:], tkf[:], op=ALU.subtract), ["tr", "tr2", "tkf"], [("tf", which)])
            sc.op("dve", lambda v, tf=tf: v.tensor_scalar(tkf[:], tf[:], 0.0, None, op0=ALU.is_lt), [("tf", which)], ["tkf"])
            sc.op("dve", lambda v, tf=tf: v.tensor_tensor(tf[:], tf[:], tkf[:], op=ALU.add), ["tkf"], [("tf", which)])
            sc.op("dve", lambda v, tf=tf: v.tensor_scalar(tf[:], tf[:], 0.0, 0.99999994, op0=ALU.max, op1=ALU.min), [], [("tf", which)])
            if which == 0:
                sc.op("act", lambda a, tf=tf: a.activation(out=tf[:], in_=tf[:], func=AF.Sin, bias=cst[:, 2:3], scale=2 * math.pi),
                      ["cst"], [("tf", which)])
                sc.op("dve", lambda v, tf=tf: v.tensor_scalar(tf[:], tf[:], prm[:, RP + 3:RP + 4], None, op0=ALU.mult), ["prm"], [("tf", which)])
                sc.dma("sp", lambda q, tf=tf, c0=c0: q.dma_start(out=SINs[:, c0:c0 + CW], in_=tf[:]), [("tf", which)], [("SINs", ci)], ("tabw", which))
            else:
                sc.op("act", lambda a, tf=tf: a.activation(out=tf[:], in_=tf[:], func=AF.Sin, bias=cst[:, 2:3], scale=2 * math.pi),
                      ["cst"], [("tf", which)])
                sc.dma("sp", lambda q, tf=tf, c0=c0: q.dma_start(out=COSs[:, c0:c0 + CW], in_=tf[:]), [("tf", which)], [("COSs", ci)], ("tabw", which))

    if stop == 'pro':
        return finish()
    sc.barrier(CVX)
    sc.dma("pool", lambda q: q.dma_start(out=WQs, in_=w_in_v[:, :, 0:D]), [], ["WQs"], "cvq")
    cv_keys = {1: [], 2: [], 3: [], 4: [], 5: []}
    for m in range(8):
        for s_, c0 in enumerate((3 * D, 4 * D, 5 * D)):
            key = ("WC1", m, s_)
            sc.dma("pool", lambda q, m=m, s_=s_, c0=c0: q.dma_start(
                out=WC1[m, :, :, s_ * 128:(s_ + 1) * 128], in_=w_in_v[:, :, c0 + m * 128:c0 + (m + 1) * 128]),
                [], [key], "cv1")
            cv_keys[1].append(key)
    wao_v = w_attn_o.rearrange("(k p) n -> p k n", p=128)
    wco_v = w_conv_o.rearrange("(k p) n -> p k n", p=128)
    wg_v = w_gate.rearrange("(k p) n -> p k n", p=128)
    for m in range(8):
        srcs = [wao_v[:, :, m * 128:(m + 1) * 128], wco_v[:, :, m * 128:(m + 1) * 128],
                wg_v[:, :, m * 128:(m + 1) * 128], wg_v[:, :, D + m * 128:D + (m + 1) * 128]]
        for s_, src in enumerate(srcs):
            key = ("WC2", m, s_)
            sc.dma("pool", lambda q, m=m, s_=s_, src=src: q.dma_start(out=WC2[m, :, :, s_ * 128:(s_ + 1) * 128], in_=src),
                   [], [key], "cv2")
            cv_keys[2].append(key)
    wout_v = w_out.rearrange("(k p) n -> p k n", p=128)
    for u in range(2):
        key = ("WC3", u)
        sc.dma("pool", lambda q, u=u: q.dma_start(out=WC3[u], in_=wout_v[:, :, u * 512:(u + 1) * 512]), [], [key], "cv3")
        cv_keys[3].append(key)
    wup_v = w_up.rearrange("(k p) n -> p k n", p=128)
    for u in range(11):
        for cc in range(2):
            for ab in range(2):
                c = 2 * u + cc
                key = ("WC4", u, cc, ab)
                o0 = (cc * 2 + ab) * 128
                sc.dma("pool", lambda q, u=u, o0=o0, ab=ab, c=c: q.dma_start(
                    out=WC4[u, :, :, o0:o0 + 128], in_=wup_v[:, :, ab * DFF + c * 128:ab * DFF + (c + 1) * 128]),
                    [], [key], "cv4")
                cv_keys[4].append(key)
    wd_v = w_down.rearrange("(c p) n -> p c n", p=128)
    for m in range(8):
        key = ("WC5", m)
        sc.dma("pool", lambda q, m=m: q.dma_start(out=WC5[m], in_=wd_v[:, :, m * 128:(m + 1) * 128]), [], [key], "cv5")
        cv_keys[5].append(key)
    for i_ in range(1, 6):
        sc.finalize_group(cv_keys[i_], "cv%d" % i_)
    wa_keys = []
    for g in range(4):
        key = ("WAs", g)
        sc.dma("pool", lambda q, g=g: q.dma_start(out=WAs[g], in_=w_ada_v[:, :, (g + 2) * D:(g + 3) * D]), [], [key], "cva")
        wa_keys.append(key)
    sc.finalize_group(wa_keys, "cva")


    KVA = Arena(nc, KV0, KV1, "kv")
    KT = KVA.alloc("KT", [128, KC, S], BF16)
    Vt = KVA.alloc("V", [128, NKT, D], BF16)
    xt = [RA.alloc("xt0", [128, D], F32), RA.alloc("xt1", [128, D], F32)]
    junk = RA.alloc("junk", [128, D], BF16)
    hT = [RA.alloc("hT0", [128, KC, TB], BF16), RA.alloc("hT1", [128, KC, TB], BF16)]
    cosb = RA.alloc("cosb", [128, TB], F32)
    sinb = RA.alloc("sinb", [128, TB], F32)
    kb = [RA.alloc("kb0", [128, TB], BF16), RA.alloc("kb1", [128, TB], BF16)]
    t1 = RA.alloc("t1", [128, TB], F32)
    t2 = RA.alloc("t2", [128, TB], F32)
    t1b = RA.alloc("t1b", [128, TB], F32)
    T1CUR[0], T1CUR[1] = t1, t2
    stat = RA.alloc("stat", [128, 8], F32)


    def rope_a(psrc_bank, kbi, copy_eng="act", t1o=None):
        k1, k2 = ROPE_KEYS
        kbt = kb[kbi]
        t1 = T1CUR[0]
        if t1o is not None:
            t1, k1 = t1o
        if copy_eng == "act":
            sc.op("act", lambda a: a.activation(out=kbt[:], in_=bank(psrc_bank), func=AF.Copy), [pk_(psrc_bank)], [("kb", kbi)])
        else:
            sc.op("dve", lambda v: v.tensor_copy(kbt[:], bank(psrc_bank)), [pk_(psrc_bank)], [("kb", kbi)])
        sc.op("dve", lambda v: v.tensor_tensor(t1[:], bank(psrc_bank), cosb[:], op=ALU.mult), [pk_(psrc_bank), "cosb"], [k1])

    def rope_b(pperm_bank, dst_ap, dst_key, kbi, t1o=None):
        k1, k2 = ROPE_KEYS
        kbt = kb[kbi]
        t1, t2 = T1CUR[0], T1CUR[1]
        if t1o is not None:
            t1, k1 = t1o
        sc.op("pe", lambda p: p.matmul(bank(pperm_bank), perm_b[:], kbt[:], start=True, stop=True), [("kb", kbi), "perm_b"], [pk_(pperm_bank)])
        sc.op("dve", lambda v: v.tensor_tensor(t2[:], bank(pperm_bank), sinb[:], op=ALU.mult), [pk_(pperm_bank), "sinb"], [k2])
        sc.op("dve", lambda v: v.tensor_tensor(dst_ap, t1[:], t2[:], op=ALU.add), [k1, k2], [dst_key])

    def rope_chunk(psrc_bank, pperm_bank, dst_ap, dst_key, kbi, copy_eng="act"):
        rope_a(psrc_bank, kbi, copy_eng)
        rope_b(pperm_bank, dst_ap, dst_key, kbi)

    def load_tables(t0):
        sc.dma("sp", lambda q: q.dma_start(out=cosb[:], in_=COSs[:, t0:t0 + TB]), [("COSs", t0 // 512)], ["cosb"], "cosb")
        sc.dma("sp", lambda q: q.dma_start(out=sinb[:], in_=SINs[:, t0:t0 + TB]), [("SINs", t0 // 512)], ["sinb"], "sinb")

    tile_ctr = [0]
    pend_rope = []

    def prep_steps(i):
        t0 = i * TB
        hb = i % 2
        hTb = hT[hb]
        steps = []
        for j in range(4):
            xs = tile_ctr[0] % 2
            tile_ctr[0] += 1

            def stepA0(j=j, xs=xs):
                xtt = xt[xs]
                r0 = t0 + j * 128
                sc.dma("act", lambda q: q.dma_start(out=xtt[:], in_=x[r0:r0 + 128, :]), [], [("xt", xs)], ("xt", xs))

            def stepA(j=j, xs=xs):
                xtt = xt[xs]
                sc.op("act", lambda a: a.memzero(stat[:, 0:1]), [], ["stat"])
                sc.op("act", lambda a: a.activation(out=junk[:], in_=xtt[:], func=AF.Square, accum_out=stat[:, 0:1]), [("xt", xs)], ["junk", "stat"])
                sc.op("act", lambda a: a.activation(out=stat[:, 1:2], in_=stat[:, 0:1], func=AF.Ln, scale=1.0 / D, bias=cst[:, 0:1]), ["stat", "cst"], ["stat"])
                sc.op("act", lambda a: a.activation(out=stat[:, 2:3], in_=stat[:, 1:2], func=AF.Exp, scale=-0.5), ["stat"], ["stat"])
                sc.op("dve", lambda v: v.tensor_scalar(xtt[:], xtt[:], stat[:, 2:3], None, op0=ALU.mult), ["stat"], [("xt", xs)])

            def stepB(j=j, xs=xs):
                xtt = xt[xs]
                for k in range(KC):
                    b_ = k // 4
                    sc.op("pe", lambda p, b_=b_, k=k: p.transpose(ps[:, b_ * 512 + (k % 4) * 128: b_ * 512 + (k % 4 + 1) * 128],
                                                                 xtt[:, k * 128:(k + 1) * 128], ident[:]),
                          [("xt", xs), "ident"], [pk_(b_)])
                for k in range(KC):
                    b_ = k // 4
                    if b_ == 0:
                        sc.op("dve", lambda v, b_=b_, k=k: v.tensor_scalar(
                            hTb[:, k, j * 128:(j + 1) * 128], ps[:, b_ * 512 + (k % 4) * 128: b_ * 512 + (k % 4 + 1) * 128],
                            gs1[:, k:k + 1], SH1(k), op0=ALU.mult, op1=ALU.add),
                            [pk_(b_), "gs1", "mod"], [("hT", hb, 0)])
                    else:
                        sc.op("act", lambda a, b_=b_, k=k: a.activation(
                            out=hTb[:, k, j * 128:(j + 1) * 128], in_=ps[:, b_ * 512 + (k % 4) * 128: b_ * 512 + (k % 4 + 1) * 128],
                            func=AF.Identity, scale=gs1[:, k:k + 1], bias=SH1(k)),
                            [pk_(b_), "gs1", "mod"], [("hT", hb, 1)])
                if j == 3:
                    sc.dma("sp", lambda q: q.dma_start(out=Hs[:, :, t0:t0 + TB], in_=hTb[:]), [("hT", hb, 0), ("hT", hb, 1)], [("Hs", i)], ("hsw", hb))
            steps.append((stepA0, stepA, stepB))
        return steps

    def kv_steps(i):
        t0 = i * TB
        hb = i % 2
        hTb = hT[hb]
        steps = []
        for m in range(KC):
            def kstep(m=m):
                if m == 0 and i == 0:
                    load_tables(t0)
                pb = 4 + (m % 2)
                for k in range(KC):
                    sc.op("pe", lambda p, k=k: p.matmul(bank(pb), WKV[:, k, m * 128:(m + 1) * 128], hTb[:, k, :],
                                                       start=(k == 0), stop=(k == KC - 1)),
                          [("WKV", 0), ("hT", hb, 0), ("hT", hb, 1)], [pk_(pb)], inc=(k == KC - 1))
                t1sel = [(t1, "t1"), (t1b, "t1b")]
                rope_a(pb, m % 2, "act", t1sel[m % 2])
                if m >= 1:
                    rope_b(6 + ((m - 1) % 2), KT[:, m - 1, t0:t0 + TB], "KT", (m - 1) % 2, t1sel[(m - 1) % 2])
                if m == KC - 1:
                    pend_rope.append(lambda: rope_b(6 + (m % 2), KT[:, m, t0:t0 + TB], "KT", m % 2, t1sel[m % 2]))
            steps.append(kstep)
        for j in range(4):
            for nh in range(2):
                def vstep(j=j, nh=nh):
                    while pend_rope:
                        pend_rope.pop()()
                    if j == 0 and nh == 0 and i + 1 < NB:
                        load_tables(t0 + TB)
                    pb = 2 + nh
                    for k in range(KC):
                        sc.op("pe", lambda p, k=k: p.matmul(bank(pb), hTb[:, k, j * 128:(j + 1) * 128],
                                                           WKV[:, k, D + nh * 512:D + (nh + 1) * 512],
                                                           start=(k == 0), stop=(k == KC - 1)),
                              [("WKV", 1), ("hT", hb, 0), ("hT", hb, 1)], [pk_(pb)], inc=(k == KC - 1))
                    sc.op("act", lambda a: a.activation(out=Vt[:, i * 4 + j, nh * 512:(nh + 1) * 512], in_=bank(pb), func=AF.Copy),
                          [pk_(pb)], ["V"])
                steps.append(vstep)
        return steps

    allprep = [st_ for i_ in range(NB) for st_ in prep_steps(i_)]
    NT = len(allprep)
    allprep[0][0]()
    for g_ in range(4):
        if g_ + 1 < NT:
            allprep[g_ + 1][0]()
        allprep[g_][1]()
        allprep[g_][2]()
    for i in range(NB):
        ks = kv_steps(i)
        ki = 0
        for pj in range(4):
            g_ = 4 * (i + 1) + pj
            if g_ + 1 < NT:
                allprep[g_ + 1][0]()
            if g_ < NT:
                allprep[g_][1]()
            for _ in range(4):
                ks[ki]()
                ki += 1
            if g_ < NT:
                allprep[g_][2]()

    if stop == 'A':
        return finish()
    sc.barrier(CVX)

    RB = Arena(nc, KV1, TOP, "rb")
    WQ = RB.alloc("WQ", [128, KC, D], BF16)
    hTB1 = RB.alloc("hT0", [128, KC, TB], BF16)
    QT2 = [RB.alloc("QT0", [128, KC, TB], BF16), RB.alloc("QT1", [128, KC, TB], BF16)]
    cosb = RB.alloc("cosb", [128, TB], F32)
    sinb = RB.alloc("sinb", [128, TB], F32)
    kb = [RB.alloc("kb0", [128, TB], BF16), RB.alloc("kb1", [128, TB], BF16)]
    Eb = [RB.alloc("E%d" % e_, [128, 2 * TB], BF16) for e_ in range(3)]
    R1 = RB.alloc("R1", [128, TB], F32)
    R2 = RB.alloc("R2", [128, TB], F32)
    RS = RB.alloc("RS", [128, TB], F32)
    t1, t2 = R2, RS
    T1CUR[0], T1CUR[1] = t1, t2
    ROPE_KEYS[0], ROPE_KEYS[1] = "R2", "RS"
    Thead = RB.alloc("Thead", [128, 2 * TB], F32)
    U1s = RB.alloc("U1s", [128, TB], F32)
    U2s = RB.alloc("U2s", [128, TB], F32)
    Tsum = [RB.alloc("Ts0", [128, 2 * TB], BF16), RB.alloc("Ts1", [128, 2 * TB], BF16)]
    Ast = [RB.alloc("Ast0", [128, TB], BF16), RB.alloc("Ast1", [128, TB], BF16)]
    O2b = RB.alloc("O2b", [128, TB], BF16)

    sc.dma("sp", lambda q: q.dma_start(out=WQ[:], in_=WQs), ["WQs"], ["WQ"], "wq")
    sctr = [0]
    resv = [None]

    def next_slot():
        if resv[0] is not None:
            return 1 - resv[0]
        sl = sctr[0] % 2
        sctr[0] += 1
        return sl

    def load_hT(i):
        t0_ = i * TB
        sc.dma("sp", lambda q: q.dma_start(out=hTB1[:], in_=Hs[:, :, t0_:t0_ + TB]), [("Hs", i)], ["hTB"], "hTB")
        load_tables(t0_)

    def q_mm(i, m):
        sl = next_slot()
        resv[0] = sl
        b0 = 2 * sl
        for k in range(KC):
            sc.op("pe", lambda p, k=k: p.matmul(bank(b0), WQ[:, k, m * 128:(m + 1) * 128], hTB1[:, k, :], start=(k == 0), stop=(k == KC - 1)),
                  ["WQ", "hTB"], [pk_(b0)], inc=(k == KC - 1))
        return b0

    def q_rope(i, m, b0):
        QTd = QT2[i % 2]
        rope_chunk(b0, b0 + 1, QTd[:, m, :], ("QT", i % 2, m), m % 2, copy_eng="dve")
        resv[0] = None

    def q_part(i, m, k0, k1_):
        for k in range(k0, k1_):
            sc.op("pe", lambda p, k=k: p.matmul(bank(6), WQ[:, k, m * 128:(m + 1) * 128], hTB1[:, k, :], start=(k == 0), stop=(k == KC - 1)),
                  ["WQ", "hTB"], [pk_(6)], inc=(k == k1_ - 1))

    def q_chunk(i, m):
        QTd = QT2[i % 2]
        q_part(i, m, 0, KC)
        rope_chunk(6, 7, QTd[:, m, :], ("QT", i % 2, m), m % 2, copy_eng="dve")

    load_hT(0)
    for m in range(KC):
        q_chunk(0, m)

    for i in range(NB):
        t0 = i * TB
        QT = QT2[i % 2]
        qk_ = lambda m: ("QT", i % 2, m)
        if i + 1 < NB:
            load_hT(i + 1)

        def qk_exp(m, kt):
            eb = (m * NKT + kt) % 3
            E = Eb[eb]
            sl = next_slot()
            b0 = 2 * sl
            sc.op("pe", lambda p: p.matmul(bank(b0), KT[0:64, m, kt * 128:(kt + 1) * 128], QT[0:64, m, :], start=True, stop=True),
                  ["KT", qk_(m)], [pk_(b0)], inc=False)
            sc.op("pe", lambda p: p.matmul(bank(b0 + 1), KT[64:128, m, kt * 128:(kt + 1) * 128], QT[64:128, m, :], start=True, stop=True),
                  ["KT", qk_(m)], [pk_(b0 + 1)])
            sc.op("act", lambda a: a.activation(out=E[:], in_=ps[:, b0 * 512:(b0 + 2) * 512], func=AF.Exp, scale=0.125), [pk_(b0), pk_(b0 + 1)], [("E", eb)])

        GS = min(8, NKT)
        NG = NKT // GS

        def pv(m, kt):
            eb = (m * NKT + kt) % 3
            E = Eb[eb]
            st, sp_ = (kt == 0), (kt == NKT - 1)
            sc.op("pe", lambda p: p.matmul(bank(4), Vt[:, kt, m * 128:(m + 1) * 128], E[:, 0:512], start=st, stop=sp_),
                  ["V", ("E", eb)], [pk_(4)], inc=False)
            sc.op("pe", lambda p: p.matmul(bank(5), Vt[:, kt, m * 128:(m + 1) * 128], E[:, 512:1024], start=st, stop=sp_),
                  ["V", ("E", eb)], [pk_(5)])

        def dsum(m, kt):
            eb = (m * NKT + kt) % 3
            E = Eb[eb]
            g, r = kt // GS, kt % GS
            gi = (m * NG + g) % 2
            T = Tsum[gi]
            if r == 0:
                sc.op("dve", lambda v: v.tensor_copy(T[:], E[:]), [("E", eb)], [("Ts", gi)])
            else:
                sc.op("dve", lambda v: v.tensor_tensor(T[:], T[:], E[:], op=ALU.add), [("E", eb)], [("Ts", gi)])
            if r == GS - 1:
                if g == 0:
                    sc.op("dve", lambda v: v.tensor_copy(Thead[:], T[:]), [("Ts", gi)], ["Thead"])
                else:
                    sc.op("dve", lambda v: v.tensor_tensor(Thead[:], Thead[:], T[:], op=ALU.add), [("Ts", gi)], ["Thead"])

        def pvs(m, kt):
            pv(m, kt)
            if kt == NKT - 1:
                sc.op("dve", lambda v: v.tensor_copy(U1s[:], bank(4)), [pk_(4)], ["U1s"])
                sc.op("dve", lambda v: v.tensor_copy(U2s[:], bank(5)), [pk_(5)], ["U2s"])
            dsum(m, kt)

        def epi1(m):
            pass

        def epi1a():
            sc.op("pe", lambda p: p.matmul(bank(6), ones_f[:], Thead[:, 0:512], start=True, stop=True), ["ones_f", "Thead"], [pk_(6)])
            sc.op("pe", lambda p: p.matmul(bank(7), ones_f[:], Thead[:, 512:1024], start=True, stop=True), ["ones_f", "Thead"], [pk_(7)])
            sc.op("act", lambda a: a.activation(out=R1[:], in_=bank(6), func=AF.Ln), [pk_(6)], ["R1"])
            sc.op("act", lambda a: a.activation(out=R2[:], in_=bank(7), func=AF.Ln), [pk_(7)], ["R2"])
            sc.op("act", lambda a: a.activation(out=R1[:], in_=R1[:], func=AF.Exp, scale=-1.0), [], ["R1"])
            sc.op("act", lambda a: a.activation(out=R2[:], in_=R2[:], func=AF.Exp, scale=-1.0), [], ["R2"])

        def epi1b():
            sc.op("dve", lambda v: v.tensor_tensor(U1s[:], U1s[:], R1[:], op=ALU.mult), ["R1"], ["U1s"])
            sc.op("dve", lambda v: v.scalar_tensor_tensor(out=U2s[:], in0=U2s[:], scalar=lamt[:, 3:4], in1=R2[:], op0=ALU.mult, op1=ALU.mult),
                  ["R2", "lamt"], ["U2s"])
            sc.op("pool", lambda g: g.tensor_tensor(R1[:], U1s[:], U2s[:], op=ALU.subtract), ["U1s", "U2s"], ["R1"])
            sc.op("pool", lambda g: g.tensor_tensor(O2b[:], R1[:], R1[:], op=ALU.mult), ["R1"], ["O2b"])

        def epi2(m):
            ab_ = m % 2
            pb = 6
            sc.op("pe", lambda p: p.matmul(bank(pb), ones_b[:], O2b[:], start=True, stop=True), ["ones_b", "O2b"], [pk_(pb)])
            sc.op("act", lambda a: a.activation(out=RS[:], in_=bank(pb), func=AF.Ln, scale=1.0 / 128, bias=cst[:, 1:2]), [pk_(pb), "cst"], ["RS"])
            sc.op("act", lambda a: a.activation(out=RS[:], in_=RS[:], func=AF.Exp, scale=-0.5), [], ["RS"])
            sc.op("dve", lambda v: v.scalar_tensor_tensor(out=Ast[ab_][:], in0=R1[:], scalar=gsub[:], in1=RS[:], op0=ALU.mult, op1=ALU.mult),
                  ["R1", "RS", "gsub"], [("Ast", ab_)])
            sc.dma("sp", lambda q: q.dma_start(out=As[:, m, t0:t0 + TB], in_=Ast[ab_][:]), [("Ast", ab_)], [("As", i, m)], ("asw", ab_))

        pending = None
        big = NKT >= 32
        H_E1A, H_E1B, H_E2 = (3, 6, 14) if big else (2, 3, 3)
        its = [(m, kt) for m in range(KC) for kt in range(NKT)]
        for idx, (m, kt) in enumerate(its):
            qk_exp(m, kt)
            if idx >= 2:
                pm, pkt = its[idx - 2]
                pvs(pm, pkt)
                if pkt == NKT - 1:
                    pending = pm
            if pending is not None:
                if kt == H_E1A:
                    epi1a()
                if kt == H_E1B:
                    epi1b()
                if kt == H_E2:
                    epi2(pending)
                    pending = None
            if i + 1 < NB:
                QTd = QT2[(i + 1) % 2]
                if big:
                    if 17 <= kt <= 20:
                        q_part(i + 1, m, 2 * (kt - 17), 2 * (kt - 17) + 2)
                    if kt == 21:
                        rope_a(6, m % 2, "dve")
                    if kt == 22:
                        rope_b(7, QTd[:, m, :], ("QT", (i + 1) % 2, m), m % 2)
                else:
                    if kt == 4:
                        q_part(i + 1, m, 0, KC)
                    if kt == 5:
                        rope_a(6, m % 2, "dve")
                    if kt == 6:
                        rope_b(7, QTd[:, m, :], ("QT", (i + 1) % 2, m), m % 2)
        pvs(KC - 1, NKT - 2)
        pvs(KC - 1, NKT - 1)
        epi1a()
        epi1b()
        epi2(KC - 1)

    if stop == 'B':
        return finish()
    sc.barrier(CVX)

    CA = Arena(nc, KV0, TOP, "c")
    NSLOT = 4
    wr = [CA.alloc("wr%d" % s_, [128, KC * 512], BF16) for s_ in range(NSLOT)]
    hTC = CA.alloc("hT", [128, KC, TB + 2], BF16)
    ATc = CA.alloc("AT", [128, KC, TB], BF16)
    xtok = CA.alloc("xtok", [128, 4, D], F32)
    WT = CA.alloc("WT", [128, KC, TB], BF16)
    MT = CA.alloc("MT", [128, KC, TB], BF16)
    X1 = CA.alloc("X1", [128, KC, TB + 1], F32)
    h2T = CA.alloc("h2T", [128, KC, TB], BF16)
    Gt = CA.alloc("G", [128, FC, TB], BF16)
    GBm_2 = [CA.alloc("GBm0", [128, TB], BF16), CA.alloc("GBm1", [128, TB], BF16)]
    GCs_2 = [CA.alloc("GCs0", [128, TB + 2], F32), CA.alloc("GCs1", [128, TB + 2], F32)]
    Zt_2 = [CA.alloc("Z0", [128, TB + 2], F32), CA.alloc("Z1", [128, TB + 2], F32)]
    Cv_2 = [CA.alloc("Cv0", [128, TB], F32), CA.alloc("Cv1", [128, TB], F32)]
    ga_2 = [CA.alloc("ga0", [128, TB], F32), CA.alloc("ga1", [128, TB], F32)]
    gb2_2 = [CA.alloc("gb20", [128, TB], F32), CA.alloc("gb21", [128, TB], F32)]
    tA_2 = [CA.alloc("tA0", [128, TB], F32), CA.alloc("tA1", [128, TB], F32)]
    tB__2 = [CA.alloc("tB0", [128, TB], F32), CA.alloc("tB1", [128, TB], F32)]
    SQ2 = [CA.alloc("SQ0", [128, TB], BF16), CA.alloc("SQ1", [128, TB], BF16)]
    ostg = CA.alloc("ostg", [128, 4, D], F32)
    RSc = CA.alloc("RSc", [128, TB], F32)
    Aa_2 = [CA.alloc("Aa0", [128, TB + 2], F32), CA.alloc("Aa1", [128, TB + 2], F32)]
    Bb_2 = [CA.alloc("Bb0", [128, TB + 1], F32), CA.alloc("Bb1", [128, TB + 1], F32)]
    Cc_2 = [CA.alloc("Cc0", [128, TB], F32), CA.alloc("Cc1", [128, TB], F32)]
    Sg_2 = [CA.alloc("Sg0", [128, TB], F32), CA.alloc("Sg1", [128, TB], F32)]
    x1s = CA.alloc("x1s", [128, KC, 1], F32)

    sc.op("dve", lambda v: v.memset(X1[:], 0.0), [], [("X1", m_) for m_ in range(KC)])

    wctr = [0]

    def wload(src_ap, src_keys, nbytes_cols, view):
        s_ = wctr[0] % NSLOT
        wctr[0] += 1
        dst = wr[s_][:, 0:nbytes_cols]
        sc.dma("sp", lambda q: q.dma_start(out=dst, in_=src_ap), src_keys, [("wr", s_)], ("wr", s_))
        return view(wr[s_]), ("wr", s_)

    v3 = lambda n: (lambda t: t[:, 0:KC * n].rearrange("p (k n) -> p k n", k=KC))

    def rsqrt_bcast(src_keys, sq_aps, w, scale, eps_ap, pbank):
        n = len(sq_aps)
        for q_, (ap_fn, key) in enumerate(sq_aps):
            SQ = SQ2[q_ % 2]
            sc.op("act", lambda a, ap_fn=ap_fn, SQ=SQ: a.activation(out=SQ[:, 0:w], in_=ap_fn(), func=AF.Square), [key], [("SQ", q_ % 2)])
            sc.op("pe", lambda p, q_=q_, SQ=SQ: p.matmul(ps[:, pbank * 512:pbank * 512 + w], ones_b[:], SQ[:, 0:w], start=(q_ == 0), stop=(q_ == n - 1)),
                  [("SQ", q_ % 2), "ones_b"], [pk_(pbank)])
        sc.op("act", lambda a: a.activation(out=RSc[:, 0:w], in_=ps[:, pbank * 512:pbank * 512 + w], func=AF.Ln, scale=scale, bias=eps_ap), [pk_(pbank), "cst"], ["RSc"])
        sc.op("act", lambda a: a.activation(out=RSc[:, 0:w], in_=RSc[:, 0:w], func=AF.Exp, scale=-0.5), [], ["RSc"])

    def ffn_and_out(i, w, has_new, tok_lo, hook=None):
        wstate = {}

        def bufs(c):
            return (Aa_2[c % 2], Bb_2[c % 2], Cc_2[c % 2], Sg_2[c % 2],
                    ("Aa", c % 2), ("Bb", c % 2), ("Cc", c % 2), ("Sg", c % 2))

        def up_mm(c):
            Aa, Bb, Cc, Sg, kA, kB, kC, kS = bufs(c)
            u, cc = c // 2, c % 2
            if has_new:
                if cc == 0:
                    wstate["w"] = wload(WC4[u], [k_ for k_ in cv_keys[4] if k_[1] == u], KC * 512, v3(512))
                wt, wk = wstate["w"]
                ub = 2 * (c % 2)
                for ab in range(2):
                    pb = ub + ab
                    o0 = (cc * 2 + ab) * 128
                    for k in range(KC):
                        sc.op("pe", lambda p, pb=pb, k=k, o0=o0: p.matmul(bank(pb), wt[:, k, o0:o0 + 128], h2T[:, k, :],
                                                                         start=(k == 0), stop=(k == KC - 1)),
                              [wk, ("h2T", k)], [pk_(pb)], inc=(k == KC - 1))
                sc.op("act", lambda a: a.activation(out=Aa[:, 2:TB + 2], in_=bank(ub), func=AF.Copy), [pk_(ub)], [kA])
                sc.op("act", lambda a: a.activation(out=Bb[:, 1:TB + 1], in_=bank(ub + 1), func=AF.Copy), [pk_(ub + 1)], [kB])
            else:
                sc.op("pool", lambda g: g.memset(Aa[:, 2:3], 0.0), [], [kA])
            sc.op("pool", lambda g: g.tensor_copy(Aa[:, 0:2], asave[:, c, :]), ["asave"], [kA])
            sc.op("pool", lambda g: g.tensor_copy(Bb[:, 0:1], bsave[:, c, :]), ["bsave"], [kB])
            if has_new:
                sc.op("pool", lambda g: g.tensor_copy(asave[:, c, :], Aa[:, TB:TB + 2]), [kA], ["asave"])
                sc.op("pool", lambda g: g.tensor_copy(bsave[:, c, :], Bb[:, TB:TB + 1]), [kB], ["bsave"])

        def post(c):
            Aa, Bb, Cc, Sg, kA, kB, kC, kS = bufs(c)
            FW = lambda tap: prm[:, P_FCONVW + tap * FC + c:P_FCONVW + tap * FC + c + 1]
            sc.op("dve", lambda v: v.tensor_scalar(Cc[:, 0:w], Aa[:, 0:w], FW(0), None, op0=ALU.mult), [kA, "prm"], [kC])
            sc.op("dve", lambda v: v.scalar_tensor_tensor(out=Cc[:, 0:w], in0=Aa[:, 1:w + 1], scalar=FW(1), in1=Cc[:, 0:w], op0=ALU.mult, op1=ALU.add), [kA], [kC])
            sc.op("dve", lambda v: v.scalar_tensor_tensor(out=Cc[:, 0:w], in0=Aa[:, 2:w + 2], scalar=FW(2), in1=Cc[:, 0:w], op0=ALU.mult, op1=ALU.add), [kA], [kC])
            sc.op("act", lambda a: a.activation(out=Sg[:, 0:w], in_=Cc[:, 0:w], func=AF.Silu, bias=prm[:, P_FCONVB + c:P_FCONVB + c + 1]), [kC, "prm"], [kS])
            sc.op("pool", lambda g: g.tensor_tensor(Gt[:, c, 0:w], Sg[:, 0:w], Bb[:, 0:w], op=ALU.mult), [kS, kB], [("G", c)])

        up_mm(0)
        for c in range(FC):
            if c + 1 < FC:
                up_mm(c + 1)
            post(c)
            if c == 2 and hook is not None:
                hook()
        def fstat_mm(m_):
            SQ = SQ2[m_ % 2]
            sc.op("pe", lambda p: p.matmul(ps[:, 6 * 512:6 * 512 + w], ones_b[:], SQ[:, 0:w], start=(m_ == 0), stop=(m_ == KC - 1)),
                  [("SQ", m_ % 2), "ones_b"], [pk_(6)])

        for m in range(KC):
            wt, wk = wload(WC5[m], [("WC5", m)], FC * 128, lambda t: t[:, 0:FC * 128].rearrange("p (c n) -> p c n", c=FC))
            pb = 4 + (m % 2)
            for c in range(FC):
                sc.op("pe", lambda p, pb=pb, c=c, wt=wt: p.matmul(ps[:, pb * 512:pb * 512 + w], wt[:, c, :], Gt[:, c, 0:w], start=(c == 0), stop=(c == FC - 1)),
                      [wk, ("G", c)], [pk_(pb)], inc=(c == FC - 1))
            if m >= 1:
                fstat_mm(m - 1)
            sc.op("dve", lambda v, pb=pb, m=m: v.scalar_tensor_tensor(out=X1[:, m, 0:w], in0=ps[:, pb * 512:pb * 512 + w], scalar=G2(m), in1=X1[:, m, 0:w],
                                                                       op0=ALU.mult, op1=ALU.add), [pk_(pb), "mod"], [("X1", m)])
            sc.op("act", lambda a, m=m: a.activation(out=SQ2[m % 2][:, 0:w], in_=X1[:, m, 0:w], func=AF.Square), [("X1", m)], [("SQ", m % 2)])
        fstat_mm(KC - 1)
        sc.op("act", lambda a: a.activation(out=RSc[:, 0:w], in_=ps[:, 6 * 512:6 * 512 + w], func=AF.Ln, scale=1.0 / D, bias=cst[:, 0:1]), [pk_(6), "cst"], ["RSc"])
        sc.op("act", lambda a: a.activation(out=RSc[:, 0:w], in_=RSc[:, 0:w], func=AF.Exp, scale=-0.5), [], ["RSc"])
        for m in range(KC):
            sc.op("dve", lambda v, m=m: v.scalar_tensor_tensor(out=X1[:, m, 0:w], in0=X1[:, m, 0:w], scalar=prm[:, P_FG + m:P_FG + m + 1], in1=RSc[:, 0:w],
                                                               op0=ALU.mult, op1=ALU.mult), ["RSc", "prm"], [("X1", m)])
        ntile = (w + 127) // 128
        for j in range(ntile):
            wj = min(128, w - j * 128)
            for half in range(2):
                pb = 2 * (j % 2) + half
                for mm in range(4):
                    m = half * 4 + mm
                    sc.op("pe", lambda p, pb=pb, mm=mm, m=m, j=j, wj=wj: p.transpose(ps[0:wj, pb * 512 + mm * 128:pb * 512 + (mm + 1) * 128],
                                                                                    X1[:, m, j * 128:j * 128 + wj], ident[:]),
                          [("X1", m), "ident"], [pk_(pb)])
                sc.op("act", lambda a, pb=pb, half=half, j=j, wj=wj: a.activation(out=ostg[0:wj, j, half * 512:(half + 1) * 512], in_=ps[0:wj, pb * 512:(pb + 1) * 512], func=AF.Copy),
                      [pk_(pb)], ["ostg"])
            tok0 = tok_lo + j * 128
            p0 = 0
            if tok0 < 0:
                p0 = -tok0
            if wj - p0 > 0:
                sc.dma("act", lambda q, j=j, p0=p0, wj=wj, tok0=tok0: q.dma_start(out=out[tok0 + p0:tok0 + wj, :], in_=ostg[p0:wj, j, :]),
                       ["ostg"], [("out", i, j)], "outw")

    def load_ha(i):
        t0 = i * TB
        lo = max(t0 - 1, 0)
        hi = min(t0 + TB + 1, S)
        c_lo = lo - (t0 - 1)
        sc.dma("sp", lambda q: q.dma_start(out=hTC[:, :, c_lo:c_lo + (hi - lo)], in_=Hs[:, :, lo:hi]),
               [("Hs", j_) for j_ in range(max(i - 1, 0), min(i + 2, NB))], ["hTC"], "hTC")
        sc.dma("sp", lambda q: q.dma_start(out=ATc[:], in_=As[:, :, t0:t0 + TB]), [("As", i, m) for m in range(KC)], ["ATc"], "ATc")
        if i == 0:
            sc.op("dve", lambda v: v.memset(hTC[:, :, 0:1], 0.0), [], ["hTC"])
        if i == NB - 1:
            sc.op("dve", lambda v: v.memset(hTC[:, :, TB + 1:TB + 2], 0.0), [], ["hTC"])

    def load_x(i):
        t0 = i * TB
        sc.dma("sp", lambda q: q.dma_start(out=xtok[:], in_=x[t0:t0 + TB, :].rearrange("(j p) d -> p j d", p=128)), [], ["xtok"], "xtokl")

    def c1(i):
        for m in range(KC):
            wt, wk = wload(WC1[m], [("WC1", m, s_) for s_ in range(3)], KC * 384, v3(384))
            cb = 4 * (m % 2)
            GBm, GCs, Zt, Cv = GBm_2[m % 2], GCs_2[m % 2], Zt_2[m % 2], Cv_2[m % 2]
            kG, kGC, kZ, kCv = ("GBm", m % 2), ("GCs", m % 2), ("Z", m % 2), ("Cv", m % 2)
            for s_ in range(3):
                pb = cb + s_
                for k in range(KC):
                    sc.op("pe", lambda p, pb=pb, k=k, s_=s_, wt=wt: p.matmul(bank(pb), wt[:, k, s_ * 128:(s_ + 1) * 128], hTC[:, k, 1:TB + 1],
                                                                            start=(k == 0), stop=(k == KC - 1)),
                          [wk, "hTC"], [pk_(pb)], inc=(k == KC - 1))
            for s_ in (1, 2):
                for k in range(KC):
                    sc.op("pe", lambda p, k=k, s_=s_, wt=wt: p.matmul(ps[:, (cb + 3) * 512 + (s_ - 1) * 2:(cb + 3) * 512 + (s_ - 1) * 2 + 2], wt[:, k, s_ * 128:(s_ + 1) * 128],
                                                                     hTC[:, k, 0:TB + 2:TB + 1], start=(k == 0), stop=(k == KC - 1)),
                          [wk, "hTC"], [pk_(cb + 3)], inc=(k == KC - 1))
            sc.op("act", lambda a: a.activation(out=GBm[:], in_=bank(cb), func=AF.Copy), [pk_(cb)], [kG])
            sc.op("act", lambda a: a.activation(out=GCs[:, 1:TB + 1], in_=bank(cb + 1), func=AF.Copy), [pk_(cb + 1)], [kGC])
            sc.op("act", lambda a: a.activation(out=GCs[:, 0:TB + 2:TB + 1], in_=ps[:, (cb + 3) * 512:(cb + 3) * 512 + 2], func=AF.Copy), [pk_(cb + 3)], [kGC])
            sc.op("dve", lambda v: v.tensor_tensor(Zt[:, 1:TB + 1], bank(cb + 2), GCs[:, 1:TB + 1], op=ALU.mult), [pk_(cb + 2), kGC], [kZ])
            sc.op("dve", lambda v: v.tensor_tensor(Zt[:, 0:TB + 2:TB + 1], ps[:, (cb + 3) * 512 + 2:(cb + 3) * 512 + 4], GCs[:, 0:TB + 2:TB + 1], op=ALU.mult), [pk_(cb + 3), kGC], [kZ])
            if i == 0:
                sc.op("dve", lambda v: v.memset(Zt[:, 0:1], 0.0), [], [kZ])
            if i == NB - 1:
                sc.op("dve", lambda v: v.memset(Zt[:, TB + 1:TB + 2], 0.0), [], [kZ])
            CW_ = lambda tap, m=m: prm[:, P_CONVW + tap * 8 + m:P_CONVW + tap * 8 + m + 1]
            sc.op("dve", lambda v, CW_=CW_: v.tensor_scalar(Cv[:], Zt[:, 0:TB], CW_(0), None, op0=ALU.mult), [kZ, "prm"], [kCv])
            sc.op("dve", lambda v, CW_=CW_: v.scalar_tensor_tensor(out=Cv[:], in0=Zt[:, 1:TB + 1], scalar=CW_(1), in1=Cv[:], op0=ALU.mult, op1=ALU.add), [kZ], [kCv])
            sc.op("dve", lambda v, CW_=CW_: v.scalar_tensor_tensor(out=Cv[:], in0=Zt[:, 2:TB + 2], scalar=CW_(2), in1=Cv[:], op0=ALU.mult, op1=ALU.add), [kZ], [kCv])
            sc.op("pool", lambda g, m=m: g.tensor_tensor(WT[:, m, :], GBm[:], Cv[:], op=ALU.mult), [kG, kCv], [("WT", m)])

    load_ha(0)
    c1(0)
    load_x(0)
    for g2_ in range(4):
        for half in range(2):
            wt, wk = wload(WAs[g2_][:, :, half * 512:(half + 1) * 512], [("WAs", g2_)], KC * 512, v3(512))
            for jj in range(4):
                j = 16 + g2_ * 8 + half * 4 + jj
                for k in range(KC):
                    sc.op("pe", lambda p, j=j, k=k, jj=jj, wt=wt: p.matmul(ps[:, j:j + 1], wt[:, k, jj * 128:(jj + 1) * 128], csil[:, k:k + 1],
                                                                          start=(k == 0), stop=(k == KC - 1)),
                          [wk, "csil"], [pk_(0)], inc=(k == KC - 1))
    sc.op("dve", lambda v: v.tensor_tensor(mod[:, 16:48], ps[:, 16:48], prm[:, P_BADA + 16:P_BADA + 48], op=ALU.add), [pk_(0), "prm"], ["mod"])
    sc.op("dve", lambda v: v.scalar_tensor_tensor(out=gs2[:], in0=mod[:, 32:40], scalar=1.0, in1=prm[:, P_N2G:P_N2G + 8], op0=ALU.add, op1=ALU.mult), ["mod", "prm"], ["gs2"])
    for i in range(NB):
        t0 = i * TB
        for m in range(KC):
            wt, wk = wload(WC2[m], [("WC2", m, s_) for s_ in range(4)], KC * 512, v3(512))
            rhs_of = [ATc, WT, hTC, hTC]
            cb = 4 * (m % 2)
            ga, gb2, tA, tB_ = ga_2[m % 2], gb2_2[m % 2], tA_2[m % 2], tB__2[m % 2]
            kga, kgb, ktA, ktB = ("ga", m % 2), ("gb2", m % 2), ("tA", m % 2), ("tB", m % 2)
            for s_ in range(4):
                pb = cb + s_
                for k in range(KC):
                    rhs = rhs_of[s_][:, k, :] if s_ < 2 else hTC[:, k, 1:TB + 1]
                    sc.op("pe", lambda p, pb=pb, k=k, s_=s_, wt=wt, rhs=rhs: p.matmul(bank(pb), wt[:, k, s_ * 128:(s_ + 1) * 128], rhs,
                                                                                     start=(k == 0), stop=(k == KC - 1)),
                          [wk, "ATc", ("WT", k), "hTC"], [pk_(pb)], inc=(k == KC - 1))
            sc.op("act", lambda a, m=m: a.activation(out=ga[:], in_=bank(cb + 2), func=AF.Sigmoid, bias=prm[:, P_BGATE + m:P_BGATE + m + 1]), [pk_(cb + 2), "prm"], [kga])
            sc.op("act", lambda a, m=m: a.activation(out=gb2[:], in_=bank(cb + 3), func=AF.Sigmoid, bias=prm[:, P_BGATE + 8 + m:P_BGATE + 8 + m + 1]), [pk_(cb + 3), "prm"], [kgb])
            sc.op("dve", lambda v: v.tensor_tensor(tA[:], bank(cb), ga[:], op=ALU.mult), [pk_(cb), kga], [ktA])
            sc.op("dve", lambda v: v.tensor_tensor(tB_[:], bank(cb + 1), gb2[:], op=ALU.mult), [pk_(cb + 1), kgb], [ktB])
            sc.op("pool", lambda g, m=m: g.tensor_tensor(MT[:, m, :], tA[:], tB_[:], op=ALU.add), [ktA, ktB], [("MT", m)])
        wc3 = [wload(WC3[u_], [("WC3", u_)], KC * 512, v3(512)) for u_ in range(2)]
        if i + 1 < NB:
            load_ha(i + 1)
        def stat_mm(m_):
            SQ = SQ2[m_ % 2]
            sc.op("pe", lambda p: p.matmul(bank(6), ones_b[:], SQ[:], start=(m_ == 0), stop=(m_ == KC - 1)),
                  [("SQ", m_ % 2), "ones_b"], [pk_(6)])

        for u in range(2):
            wt, wk = wc3[u]
            for mm in range(4):
                m = u * 4 + mm
                pb = mm % 2
                for k in range(KC):
                    sc.op("pe", lambda p, pb=pb, k=k, mm=mm, wt=wt: p.matmul(bank(pb), wt[:, k, mm * 128:(mm + 1) * 128], MT[:, k, :],
                                                                            start=(k == 0), stop=(k == KC - 1)),
                          [wk, ("MT", k)], [pk_(pb)], inc=(k == KC - 1))
                pt = 2 + (mm % 2)
                for j in range(4):
                    sc.op("pe", lambda p, pt=pt, j=j, m=m: p.transpose(ps[:, pt * 512 + j * 128:pt * 512 + (j + 1) * 128], xtok[:, j, m * 128:(m + 1) * 128], ident[:]),
                          ["xtok", "ident"], [pk_(pt)])
                sc.op("act", lambda a, pt=pt, m=m: a.activation(out=X1[:, m, 1:TB + 1], in_=bank(pt), func=AF.Copy), [pk_(pt)], [("X1", m)])
                if m >= 1:
                    stat_mm(m - 1)
                sc.op("dve", lambda v, pb=pb, m=m: v.scalar_tensor_tensor(out=X1[:, m, 1:TB + 1], in0=bank(pb), scalar=G1(m), in1=X1[:, m, 1:TB + 1],
                                                                           op0=ALU.mult, op1=ALU.add), [pk_(pb), "mod"], [("X1", m)])
                sc.op("act", lambda a, m=m: a.activation(out=SQ2[m % 2][:], in_=X1[:, m, 1:TB + 1], func=AF.Square), [("X1", m)], [("SQ", m % 2)])
        stat_mm(KC - 1)
        sc.op("pool", lambda g: g.tensor_copy(x1s[:], X1[:, :, TB:TB + 1]), [("X1", m_) for m_ in range(KC)], ["x1s"])
        sc.op("act", lambda a: a.activation(out=RSc[:], in_=bank(6), func=AF.Ln, scale=1.0 / D, bias=cst[:, 0:1]), [pk_(6), "cst"], ["RSc"])
        sc.op("act", lambda a: a.activation(out=RSc[:], in_=RSc[:], func=AF.Exp, scale=-0.5), [], ["RSc"])
        for m in range(KC):
            tA, ktA = tA_2[m % 2], ("tA", m % 2)
            sc.op("dve", lambda v, m=m: v.scalar_tensor_tensor(out=tA[:], in0=X1[:, m, 1:TB + 1], scalar=gs2[:, m:m + 1], in1=RSc[:], op0=ALU.mult, op1=ALU.mult),
                  [("X1", m), "RSc", "gs2"], [ktA])
            sc.op("act", lambda a, m=m: a.activation(out=h2T[:, m, :], in_=tA[:], func=AF.Identity, bias=SH2(m)), [ktA, "mod"], [("h2T", m)])
        if i + 1 < NB:
            c1(i + 1)
        ffn_and_out(i, TB, True, t0 - 1, hook=(lambda i=i: load_x(i + 1)) if i + 1 < NB else None)
        sc.op("pool", lambda g: g.tensor_copy(X1[:, :, 0:1], x1s[:]), ["x1s"], [("X1", m_) for m_ in range(KC)])
    ffn_and_out(NB, 1, False, S - 1)

    return finish()


_PROG_CACHE = {}


def _host_consts():
    ident = np.eye(128, dtype=np.float32)
    perm = np.zeros((128, 128), dtype=np.float32)
    for m in range(128):
        partner = m + 32 if (m % 64) < 32 else m - 32
        perm[partner, m] = 1.0
    inv_freq = (10000.0 ** (-np.arange(0, 64, 2, dtype=np.float32) / np.float32(64))).astype(np.float32)
    rope = np.zeros((128, 4), dtype=np.float32)
    for p in range(128):
        s = -1.0 if (p % 64) < 32 else 1.0
        rope[p, 0] = np.float32(np.float64(inv_freq[p % 32]) / (2 * math.pi))
        rope[p, 1] = s * 2 * math.pi
        rope[p, 2] = -s * math.pi
        rope[p, 3] = s
    return ident, perm, rope


def _colmajor(v, nchunk):
    return np.ascontiguousarray(np.asarray(v, dtype=np.float32).reshape(nchunk, 128).T)


def run(inputs, S, stop=None):
    inputs = {k: np.asarray(v) for k, v in inputs.items()}
    B = inputs["x"].shape[0]
    if (S, stop) not in _PROG_CACHE:
        _PROG_CACHE[(S, stop)] = build_program(S, stop)
    nc = _PROG_CACHE[(S, stop)]
    ident, perm, rope = _host_consts()
    in_maps = []
    shared = {
        "ident": ident, "perm": perm,
        "w_ada": np.ascontiguousarray(inputs["w_ada"][0], dtype=np.float32),
        "w_in": np.ascontiguousarray(inputs["w_in"][0], dtype=np.float32),
        "w_attn_o": np.ascontiguousarray(inputs["w_attn_o"][0], dtype=np.float32),
        "w_conv_o": np.ascontiguousarray(inputs["w_conv_o"][0], dtype=np.float32),
        "w_gate": np.ascontiguousarray(inputs["w_gate"][0], dtype=np.float32),
        "w_out": np.ascontiguousarray(inputs["w_out"][0], dtype=np.float32),
        "w_up": np.ascontiguousarray(inputs["w_up"][0], dtype=np.float32),
        "w_down": np.ascontiguousarray(inputs["w_down"][0], dtype=np.float32),
    }
    lamrow = np.concatenate([inputs["lambda_q1"][0], inputs["lambda_k1"][0], inputs["lambda_q2"][0], inputs["lambda_k2"][0]]).astype(np.float32)
    for b in range(B):
        prm = np.zeros((128, P_TOT), dtype=np.float32)
        prm[:, P_BADA:P_BADA + 48] = _colmajor(inputs["b_ada"][0], 48)
        prm[:, P_N1G:P_N1G + 8] = _colmajor(inputs["norm1_g"][0], 8)
        prm[:, P_N2G:P_N2G + 8] = _colmajor(inputs["norm2_g"][0], 8)
        prm[:, P_FG:P_FG + 8] = _colmajor(inputs["final_g"], 8)
        for tap in range(3):
            prm[:, P_CONVW + tap * 8:P_CONVW + (tap + 1) * 8] = _colmajor(inputs["conv_w"][0, tap], 8)
            prm[:, P_FCONVW + tap * FC:P_FCONVW + (tap + 1) * FC] = _colmajor(inputs["ffn_conv_w"][0, tap], FC)
        prm[:, P_FCONVB:P_FCONVB + FC] = _colmajor(inputs["ffn_conv_b"][0], FC)
        prm[:, P_BGATE:P_BGATE + 16] = _colmajor(inputs["b_gate"][0], 16)
        prm[:, P_SUBLN:P_SUBLN + 1] = np.asarray(inputs["subln_g"][0], dtype=np.float32).reshape(128, 1)
        prm[:, P_LAM:P_LAM + 256] = np.broadcast_to(lamrow[None, :], (128, 256))
        prm[:, P_ROPE:P_ROPE + 4] = rope
        prm[:, P_C:P_C + 8] = _colmajor(inputs["c"][b], 8)
        d = dict(shared)
        d["x"] = np.ascontiguousarray(inputs["x"][b, :S], dtype=np.float32)
        d["pos"] = np.ascontiguousarray(inputs["positions"][b, :S], dtype=np.int32)
        d["params"] = prm
        in_maps.append(d)
    res = run_bass_kernel_spmd(nc, in_maps, core_ids=list(range(B)))
    return np.stack([np.asarray(r["out"], dtype=np.float32) for r in res.results], axis=0)


def kernel(**inputs):
    return run(inputs, 4096)
```
